# Optimizing a Trainium2 kernel written in Bass

```python
import math
import jax, jax.numpy as jnp
from jax import lax
import numpy as np

D_MODEL = 1024
BATCH = 8
SEQ = 4096
DEPTH = 4
DEC_BATCH = 8
DEC_SEQ = 8192
PAST_LEN = 128

GRID_W = 64
HEAD_DIM = 64
N_Q_HEADS = 8
N_KV_HEADS = 2
Q_PER_KV = N_Q_HEADS // N_KV_HEADS
ATTN_W = N_Q_HEADS * HEAD_DIM
KV_W = N_KV_HEADS * HEAD_DIM
HYENA_W = 512
HYENA_ORDER = 2
MIX_W = ATTN_W + HYENA_W
SHORT_CONV = 3
FILT_EMB = 33
FILT_BANDS = (FILT_EMB - 1) // 2
FILT_HID = 64
N_FILT = HYENA_ORDER * 2 * HYENA_W
MIN_DECAY = math.log(100.0) / 1.5
MAX_DECAY = math.log(100.0) / 0.3
MOD_SHIFT = 0.05
ROPE_THETA = 10000.0
ROPE_FREQS = HEAD_DIM // 4
Q_BLOCK = 128
EPS = 1e-6
SPLITS = (ATTN_W, ATTN_W + KV_W, ATTN_W + 2 * KV_W, 2 * ATTN_W + 2 * KV_W,
          2 * ATTN_W + 2 * KV_W + (HYENA_ORDER + 1) * HYENA_W)
D_IN_PROJ = SPLITS[-1] + HYENA_W

kernel_name = "hymba_gqa_axialrope_hyena_encoder"


def rms_norm(x, w):
    xf = x.astype(jnp.float32)
    y = xf * lax.rsqrt(jnp.mean(xf * xf, axis=-1, keepdims=True) + EPS)
    return (y * w.astype(jnp.float32)).astype(x.dtype)


def axial_rope_tables(L):
    rows = L // GRID_W
    r, c = jnp.meshgrid(jnp.arange(rows), jnp.arange(GRID_W), indexing="ij")
    pos = jnp.stack([r.reshape(-1), c.reshape(-1)], axis=-1).astype(jnp.float32)
    freqs = ROPE_THETA ** (-jnp.arange(ROPE_FREQS, dtype=jnp.float32) / ROPE_FREQS)
    ang = pos[:, :, None] * freqs
    return jnp.cos(ang), jnp.sin(ang)


def apply_axial_rope(x, cos, sin):
    B, L, H, _ = x.shape
    xr = x.reshape(B, L, H, 2, 2, ROPE_FREQS).astype(jnp.float32)
    x1, x2 = xr[..., 0, :], xr[..., 1, :]
    c, s = cos[None, :, None], sin[None, :, None]
    out = jnp.stack([x1 * c - x2 * s, x2 * c + x1 * s], axis=-2)
    return out.reshape(B, L, H, HEAD_DIM).astype(x.dtype)


def blocked_gqa_attention(q, k, v):
    B, L, _, d = q.shape
    nb = L // Q_BLOCK
    scale = 1.0 / math.sqrt(d)
    qb = q.reshape(B, nb, Q_BLOCK, N_KV_HEADS, Q_PER_KV, d).transpose(1, 0, 2, 3, 4, 5)

    def one_block(qi):
        s = jnp.einsum("bqhgd,bkhd->bhgqk", qi, k, preferred_element_type=jnp.float32) * scale
        p = jax.nn.softmax(s, axis=-1)
        return jnp.einsum("bhgqk,bkhd->bqhgd", p.astype(v.dtype), v)

    o = lax.map(one_block, qb)
    return o.transpose(1, 0, 2, 3, 4, 5).reshape(B, L, N_Q_HEADS * d)


def centred_short_conv(u, w, b):
    L = u.shape[1]
    up = jnp.pad(u, ((0, 0), (1, 1), (0, 0)))
    return up[:, :L] * w[0] + up[:, 1:L + 1] * w[1] + up[:, 2:] * w[2] + b


def hyena_filters_fft(L, w1, b1, w2, b2, w3, freq, decay):
    f32 = jnp.float32
    t = jnp.linspace(0.0, 1.0, L, dtype=f32)[:, None]
    w = 2.0 * math.pi * jnp.arange(L, dtype=f32) / L
    bands = jnp.linspace(1e-4, FILT_BANDS - 1, FILT_BANDS, dtype=f32)
    ang = w[:, None] * bands[None, :]
    z = jnp.concatenate([t, jnp.cos(ang), -jnp.sin(ang)], axis=-1)
    fr = freq.astype(f32)
    h = jnp.sin(fr * (z @ w1.astype(f32) + b1.astype(f32)))
    h = jnp.sin(fr * (h @ w2.astype(f32) + b2.astype(f32)))
    h = h @ w3.astype(f32)
    h = h * (jnp.exp(-t * jnp.abs(decay.astype(f32))) + MOD_SHIFT)
    h = h.reshape(L, HYENA_ORDER, 2, HYENA_W)
    h_fwd, h_bwd = h[:, :, 0], h[:, :, 1]
    buf = jnp.concatenate([h_fwd, jnp.zeros_like(h_fwd[:1]), h_bwd[1:][::-1]], axis=0)
    buf = buf / jnp.sum(jnp.abs(buf), axis=0, keepdims=True)
    return jnp.fft.rfft(buf, axis=0)


def long_conv(u, h_f, bias):
    L = u.shape[1]
    uf = u.astype(jnp.float32)
    y = jnp.fft.irfft(jnp.fft.rfft(uf, n=2 * L, axis=1) * h_f[None], n=2 * L, axis=1)[:, :L]
    return (y + uf * bias.astype(jnp.float32)).astype(u.dtype)


def run_trunk(x, norm_w, w_in, q_norm_w, k_norm_w, conv_w, conv_b, filt_w1, filt_b1,
              filt_w2, filt_b2, filt_w3, filt_freq, filt_decay, hyena_bias,
              attn_out_norm_w, hyena_out_norm_w, w_out, final_norm_w):
    B, L, _ = x.shape
    cos, sin = axial_rope_tables(L)
    for l in range(DEPTH):
        h = rms_norm(x, norm_w[l])
        proj = h @ w_in[l]
        q, k, v, g_a, u_h, g_h = jnp.split(proj, SPLITS, axis=-1)
        q = apply_axial_rope(rms_norm(q.reshape(B, L, N_Q_HEADS, HEAD_DIM), q_norm_w[l]), cos, sin)
        k = apply_axial_rope(rms_norm(k.reshape(B, L, N_KV_HEADS, HEAD_DIM), k_norm_w[l]), cos, sin)
        v = v.reshape(B, L, N_KV_HEADS, HEAD_DIM)
        attn = blocked_gqa_attention(q, k, v)
        o_a = rms_norm(attn, attn_out_norm_w[l]) * jax.nn.silu(g_a)
        u = centred_short_conv(u_h, conv_w[l], conv_b[l])
        v_h, x1, x2 = jnp.split(u, HYENA_ORDER + 1, axis=-1)
        hf = hyena_filters_fft(L, filt_w1[l], filt_b1[l], filt_w2[l], filt_b2[l],
                               filt_w3[l], filt_freq[l], filt_decay[l])
        zz = x1 * long_conv(v_h, hf[:, 0], hyena_bias[l, 0])
        zz = x2 * long_conv(zz, hf[:, 1], hyena_bias[l, 1])
        o_h = rms_norm(zz, hyena_out_norm_w[l]) * jax.nn.silu(g_h)
        x = x + jnp.concatenate([o_a, o_h], axis=-1) @ w_out[l]
    return rms_norm(x, final_norm_w)


def setup_inputs(seed: int = 0) -> dict:
    key = jax.random.key(seed)
    ks = jax.random.split(key, 24)
    n = jax.random.normal
    f32 = jnp.float32
    return {
        "x_prompt": n(ks[0], (BATCH, SEQ, D_MODEL), f32),
        "x_sample": n(ks[1], (DEC_BATCH, DEC_SEQ, D_MODEL), f32),
        "norm_w": 1.0 + 0.02 * n(ks[2], (DEPTH, D_MODEL), f32),
        "w_in": n(ks[3], (DEPTH, D_MODEL, D_IN_PROJ), f32) * D_MODEL ** -0.5,
        "q_norm_w": 1.0 + 0.02 * n(ks[4], (DEPTH, HEAD_DIM), f32),
        "k_norm_w": 1.0 + 0.02 * n(ks[5], (DEPTH, HEAD_DIM), f32),
        "conv_w": n(ks[6], (DEPTH, SHORT_CONV, (HYENA_ORDER + 1) * HYENA_W), f32) * SHORT_CONV ** -0.5,
        "conv_b": 0.01 * n(ks[7], (DEPTH, (HYENA_ORDER + 1) * HYENA_W), f32),
        "filt_w1": n(ks[8], (DEPTH, FILT_EMB, FILT_HID), f32) * FILT_EMB ** -0.5,
        "filt_b1": 0.01 * n(ks[9], (DEPTH, FILT_HID), f32),
        "filt_w2": n(ks[10], (DEPTH, FILT_HID, FILT_HID), f32) * FILT_HID ** -0.5,
        "filt_b2": 0.01 * n(ks[11], (DEPTH, FILT_HID), f32),
        "filt_w3": n(ks[12], (DEPTH, FILT_HID, N_FILT), f32) * FILT_HID ** -0.5,
        "filt_freq": 1.0 + 0.1 * n(ks[13], (DEPTH, FILT_HID), f32),
        "filt_decay": jax.random.uniform(ks[14], (DEPTH, N_FILT), f32, MIN_DECAY, MAX_DECAY),
        "hyena_bias": 0.1 * n(ks[15], (DEPTH, HYENA_ORDER, HYENA_W), f32),
        "attn_out_norm_w": 1.0 + 0.02 * n(ks[16], (DEPTH, ATTN_W), f32),
        "hyena_out_norm_w": 1.0 + 0.02 * n(ks[17], (DEPTH, HYENA_W), f32),
        "w_out": n(ks[18], (DEPTH, MIX_W, D_MODEL), f32) * MIX_W ** -0.5,
        "final_norm_w": 1.0 + 0.02 * n(ks[19], (D_MODEL,), f32),
    }


def reference(x_prompt, x_sample, norm_w, w_in, q_norm_w, k_norm_w, conv_w, conv_b,
              filt_w1, filt_b1, filt_w2, filt_b2, filt_w3, filt_freq, filt_decay,
              hyena_bias, attn_out_norm_w, hyena_out_norm_w, w_out, final_norm_w):
    y_prompt = run_trunk(x_prompt, norm_w, w_in, q_norm_w, k_norm_w, conv_w, conv_b,
                         filt_w1, filt_b1, filt_w2, filt_b2, filt_w3, filt_freq, filt_decay,
                         hyena_bias, attn_out_norm_w, hyena_out_norm_w, w_out, final_norm_w)
    y_sample = run_trunk(x_sample, norm_w, w_in, q_norm_w, k_norm_w, conv_w, conv_b,
                         filt_w1, filt_b1, filt_w2, filt_b2, filt_w3, filt_freq, filt_decay,
                         hyena_bias, attn_out_norm_w, hyena_out_norm_w, w_out, final_norm_w)
    return (y_prompt, y_sample)
```

```python
import numpy as np
import ml_dtypes
from contextlib import ExitStack
import concourse.bass as bass
import concourse.mybir as mybir
from concourse.bass_utils import run_bass_kernel_spmd

F32 = mybir.dt.float32
BF16 = mybir.dt.bfloat16
ALU = mybir.AluOpType
AF = mybir.ActivationFunctionType
AX = mybir.AxisListType

ENGS = ("pe", "act", "dve", "pool", "sp")
SEM_MAXV = 30000

D_MODEL = 1024
DIN = 3328
DEPTH = 4
EPS = 1e-6
PI = float(np.pi)


class Buf:
    __slots__ = ("w", "r", "rd")

    def __init__(self):
        self.w = None
        self.r = {}
        self.rd = []


class Op:
    __slots__ = ("eng", "fn", "deps", "sig", "dma", "sem", "val", "prev")

    def __init__(self, eng, fn, dma):
        self.eng = eng
        self.fn = fn
        self.dma = dma
        self.deps = ()
        self.sig = dma
        self.sem = None
        self.val = 0
        self.prev = 0


class Sched:
    def __init__(self, nc):
        self.nc = nc
        self.ops = {e: [] for e in ENGS}
        self.last = {e: None for e in ENGS}
        self.dmas = []
        self.bufs = []
        self.esems = {e: [] for e in ENGS}
        self.ecount = {e: 0 for e in ENGS}
        nd = {"sp": 24, "pool": 16, "act": 8}
        self.dsems = {e: [nc.alloc_semaphore(name=f"d_{e}_{i}") for i in range(n)] for e, n in nd.items()}
        self.dcnt = {e: [0] * n for e, n in nd.items()}
        self.drr = {e: 0 for e in nd}
        self.seen = {e: {} for e in ENGS}
        self.n_inst = 0

    def buf(self):
        b = Buf()
        self.bufs.append(b)
        return b

    def op(self, eng, fn, reads=(), writes=(), dma=False):
        o = Op(eng, fn, dma)
        raw = set()
        deps = set()
        for b in reads:
            if b.w is not None:
                raw.add(b.w)
        for b in writes:
            if b.w is not None:
                deps.add(b.w)
            deps.update(b.r.values())
            deps.update(b.rd)
        deps -= raw
        if dma:
            dl = list(raw) + list(deps)
        else:
            dl = list(raw) + [d for d in deps if d.dma or d.eng != eng]
        for d in dl:
            d.sig = True
        o.deps = dl
        for b in reads:
            if dma:
                b.rd.append(o)
            else:
                b.r[eng] = o
        for b in writes:
            b.w = o
            b.r = {}
            b.rd = []
        self.ops[eng].append(o)
        if dma:
            self.dmas.append(o)
        else:
            self.last[eng] = o
        return o

    def flush(self):
        nc = self.nc
        lasts = [o for o in self.last.values() if o is not None]
        for e in ENGS:
            o = Op(e, None, False)
            o.deps = [d for d in lasts if d.eng != e] + list(self.dmas)
            for d in o.deps:
                d.sig = True
            self.ops[e].append(o)
        for e in ENGS:
            for o in self.ops[e]:
                if o.fn is None or not o.sig:
                    continue
                if o.dma:
                    i = self.drr[e]
                    self.drr[e] = (i + 1) % len(self.dsems[e])
                    o.sem = self.dsems[e][i]
                    o.prev = self.dcnt[e][i]
                    self.dcnt[e][i] += 16
                    o.val = self.dcnt[e][i]
                else:
                    c = self.ecount[e]
                    ep, v = divmod(c, SEM_MAXV)
                    while len(self.esems[e]) <= ep:
                        self.esems[e].append(nc.alloc_semaphore(name=f"e_{e}_{len(self.esems[e])}"))
                    o.sem = self.esems[e][ep]
                    o.val = v + 1
                    self.ecount[e] = c + 1

        def emit(e, h):
            seen = self.seen[e]
            for o in self.ops[e]:
                waits = {}
                for d in o.deps:
                    k = id(d.sem)
                    if seen.get(k, 0) >= d.val:
                        continue
                    if k not in waits or waits[k][1] < d.val:
                        waits[k] = (d.sem, d.val)
                if o.dma and o.prev > 0 and seen.get(id(o.sem), 0) < o.prev:
                    k = id(o.sem)
                    if k not in waits or waits[k][1] < o.prev:
                        waits[k] = (o.sem, o.prev)
                for k, (s, v) in waits.items():
                    h.wait_ge(s, v)
                    seen[k] = v
                    self.n_inst += 1
                if o.fn is None:
                    continue
                ins = o.fn(h)
                self.n_inst += 1
                if o.sig:
                    ins.then_inc(o.sem, 16 if o.dma else 1)

        with nc.Block() as block:
            @block.tensor
            def _(h):
                emit("pe", h)

            @block.scalar
            def _(h):
                emit("act", h)

            @block.vector
            def _(h):
                emit("dve", h)

            @block.gpsimd
            def _(h):
                emit("pool", h)

            @block.sync
            def _(h):
                emit("sp", h)
        self.ops = {e: [] for e in ENGS}
        self.last = {e: None for e in ENGS}
        self.dmas = []
        for b in self.bufs:
            b.w = None
            b.r = {}
            b.rd = []


class T:
    __slots__ = ("t", "b")

    def __init__(self, t, b):
        self.t = t
        self.b = b

    def __getitem__(self, k):
        return self.t[k]


class Env:
    pass


_uid = [0]


def uid():
    _uid[0] += 1
    return _uid[0]


class Phase:
    def __init__(self, env, name):
        self.env = env
        self.es = ExitStack()
        self.name = name

    def sb(self, shape, dt):
        t = self.es.enter_context(self.env.nc.sbuf_tensor(f"{self.name}_s{uid()}", list(shape), dt))
        return T(t, self.env.S.buf())

    def ps(self, shape, dt):
        t = self.es.enter_context(self.env.nc.psum_tensor(f"{self.name}_p{uid()}", list(shape), dt))
        return T(t, self.env.S.buf())

    def alias(self, tile):
        return T(tile.t, self.env.S.buf())

    def end(self):
        self.env.S.flush()
        self.es.close()


def mk_ops(env):
    S = env.S

    def OP(eng, fn, r=(), w=()):
        return S.op(eng, fn, [x.b for x in r], [x.b for x in w])

    def DMA(eng, out, in_, r=(), w=()):
        return S.op(eng, lambda e: e.dma_start(out=out, in_=in_), [x.b for x in r], [x.b for x in w], dma=True)

    return OP, DMA


def fft_cfg(L):
    NT = L // 128
    K1r = NT + 1
    K1 = K1r + (K1r % 2)
    CPB = min(512 // (2 * K1), 8)
    CB = CPB * 4
    for m_ in range(max(1, 28 // CPB), 0, -1):
        if (CPB * m_) % 4 == 0:
            CB = CPB * m_
            break
    batches = []
    c = 0
    while c < 512:
        n = min(CB, 512 - c)
        batches.append((c, n))
        c += n
    return NT, K1r, K1, CPB, CB, batches


def make_consts(L):
    bf = ml_dtypes.bfloat16
    NT, K1r, K1, CPB, CB, batches = fft_cfg(L)
    N1 = 2 * NT
    N = 2 * L
    c = {}
    t = np.arange(L)
    pos = np.stack([t // 64, t % 64], -1).astype(np.float32)
    freqs = (10000.0 ** (-np.arange(16, dtype=np.float32) / 16)).astype(np.float32)
    ang = pos[:, :, None] * freqs[None, None, :]
    c["ropec"] = np.cos(ang).astype(np.float32).reshape(L, 32)
    c["ropes"] = np.sin(ang).astype(np.float32).reshape(L, 32)
    tt = np.linspace(0.0, 1.0, L, dtype=np.float32)
    w = (2.0 * np.pi * np.arange(L, dtype=np.float32) / L).astype(np.float32)
    bands = np.linspace(1e-4, 15.0, 16, dtype=np.float32)
    a2 = (w[:, None] * bands[None, :]).astype(np.float32)
    z = np.concatenate([tt[:, None], np.cos(a2), -np.sin(a2)], -1).astype(np.float32)
    c["zT"] = np.ascontiguousarray(z.T)
    c["tv"] = tt[None, :].copy()
    n1 = np.arange(NT)[:, None].astype(np.float64)
    k1 = np.arange(K1)[None, :].astype(np.float64)
    valid = (np.arange(K1) < K1r).astype(np.float64)[None, :]
    a = 2 * np.pi * n1 * k1 / N1
    c["F1"] = np.concatenate([np.cos(a) * valid, -np.sin(a) * valid], 1).astype(bf)
    n2 = np.arange(128)[:, None].astype(np.float64)
    a = 2 * np.pi * n2 * k1 / N
    c["twr"] = (np.cos(a) * valid).astype(bf)
    c["twi"] = (-np.sin(a) * valid).astype(bf)
    k2 = np.arange(128)[None, :].astype(np.float64)
    a = 2 * np.pi * n2 * k2 / 128
    C2, S2 = np.cos(a), np.sin(a)
    c["c2"] = np.stack([C2, S2, -S2, -C2], 1).astype(bf)
    c["e12"] = np.stack([np.concatenate([C2, S2], 1), np.concatenate([-S2, C2], 1), np.concatenate([-C2, -S2], 1)], 1).astype(bf)
    a = 2 * np.pi * k1.T * n2.T / N
    c["tir"] = (np.cos(a) * valid.T).astype(bf)
    c["tii"] = (np.sin(a) * valid.T).astype(bf)
    wk = np.full((K1, 1), 2.0)
    wk[0, 0] = 1.0
    wk[NT, 0] = 1.0
    wk = wk * valid.T
    a = 2 * np.pi * k1.T * n1.T / N1
    c1w = np.zeros((128, 3, NT))
    c1w[:K1] = np.stack([wk * np.cos(a) / N, -wk * np.sin(a) / N, -wk * np.cos(a) / N], 1)
    c["c1w"] = c1w.astype(bf)
    return c


CONST_SHAPES = None


def phase_prep(env, depth):
    OP, DMA = mk_ops(env)
    ph = Phase(env, "prep")
    wbf = ph.sb([128, 8, DIN], BF16)
    wb = [ph.alias(wbf) for _ in range(8)]
    wob = ph.sb([128, 8, D_MODEL], BF16)
    wbb = [ph.alias(wob) for _ in range(8)]
    nws = [ph.sb([128, 8], F32) for _ in range(2)]
    stg = [ph.sb([128, DIN], F32) for _ in range(3)]
    stq = [ph.alias(s_) for s_ in stg]
    sto = [ph.sb([128, D_MODEL], F32) for _ in range(2)]
    k = 0
    for l in range(depth):
        nw = nws[l % 2]
        DMA("sp", nw[:], env.norm_w[l, :].rearrange("(c p) -> p c", p=128), w=[nw])
        for dc in range(8):
            st, sq_ = stg[k % 3], stq[k % 3]
            k += 1
            src = env.w_in[l, dc * 128:(dc + 1) * 128, :]
            DMA("sp", st[:, 0:512].rearrange("p (j hh d) -> p j hh d", j=4, hh=2)[:, :, 0, :],
                src[:, 0:512].rearrange("p (hh j d) -> p hh j d", hh=2, j=4)[:, 0, :, :], w=[sq_])
            DMA("sp", st[:, 0:512].rearrange("p (j hh d) -> p j hh d", j=4, hh=2)[:, :, 1, :],
                src[:, 0:512].rearrange("p (hh j d) -> p hh j d", hh=2, j=4)[:, 1, :, :], w=[sq_])
            DMA("pool", st[:, 512:DIN], src[:, 512:DIN], w=[st])
            if dc % 8 in (0, 3, 5):
                OP("act", lambda e, dc=dc, st=st, nw=nw: e.activation(out=wbf[:, dc, :], in_=st[:], func=AF.Copy, scale=nw[:, dc:dc + 1]),
                   r=[st, sq_, nw], w=[wb[dc]])
            else:
                eng = "dve" if dc % 8 in (1, 4, 6, 7) else "pool"
                OP(eng, lambda e, dc=dc, st=st, nw=nw: e.tensor_scalar(out=wbf[:, dc, :], in0=st[:], scalar1=nw[:, dc:dc + 1],
                                                                       scalar2=None, op0=ALU.mult), r=[st, sq_, nw], w=[wb[dc]])
            DMA("sp", env.winb[l, :, dc, :], wbf[:, dc, :], r=[wb[dc]])
        for dc in range(8):
            st = sto[dc % 2]
            DMA("pool", st[:], env.w_out[l, dc * 128:(dc + 1) * 128, :], w=[st])
            eng = "dve" if dc % 2 == 0 else "act"
            if eng == "dve":
                OP("dve", lambda e, dc=dc, st=st: e.tensor_copy(out=wob[:, dc, :], in_=st[:]), r=[st], w=[wbb[dc]])
            else:
                OP("act", lambda e, dc=dc, st=st: e.activation(out=wob[:, dc, :], in_=st[:], func=AF.Copy), r=[st], w=[wbb[dc]])
            DMA("sp", env.woutb[l, :, dc, :], wob[:, dc, :], r=[wbb[dc]])
    ph.end()


def phase_inproj(env, sd, l):
    nc, S = env.nc, env.S
    OP, DMA = mk_ops(env)
    L, NT = sd.L, sd.NT
    x_src = sd.x_in if l == 0 else sd.xres
    ph = Phase(env, "p1")
    wbf = ph.sb([128, 8, DIN], BF16)
    wb = [ph.alias(wbf) for _ in range(8)]
    for dc in range(8):
        DMA("sp" if dc % 2 == 0 else "act", wbf[:, dc, :], env.winb[l, :, dc, :], w=[wb[dc]])
    cosT = ph.sb([128, NT, 32], F32)
    sinT = ph.sb([128, NT, 32], F32)
    DMA("sp", cosT[:], sd.c["ropec"].rearrange("(t p) k -> p t k", p=128), w=[cosT])
    DMA("sp", sinT[:], sd.c["ropes"].rearrange("(t p) k -> p t k", p=128), w=[sinT])
    tq = ph.sb([128, 64], F32)
    tk = ph.sb([128, 64], F32)
    DMA("sp", tq[:], env.q_norm_w[l:l + 1, :].partition_broadcast(128), w=[tq])
    DMA("sp", tk[:], env.k_norm_w[l:l + 1, :].partition_broadcast(128), w=[tk])
    wqk = ph.sb([128, 10, 64], F32)
    OP("dve", lambda e: e.tensor_scalar(out=wqk[:, 0:8, :], in0=tq[:].unsqueeze(1).to_broadcast([128, 8, 64]),
                                        scalar1=0.125, scalar2=None, op0=ALU.mult), r=[tq], w=[wqk])
    OP("dve", lambda e: e.tensor_copy(out=wqk[:, 8:10, :], in_=tk[:].unsqueeze(1).to_broadcast([128, 2, 64])), r=[tk], w=[wqk])
    identb = env.identb
    xsets = [[ph.sb([128, D_MODEL], F32) for _ in range(4)] for _ in range(2)]
    sqscr = ph.sb([128, D_MODEL], F32)
    ssx = [ph.sb([128, 4], F32) for _ in range(2)]
    rs = [ph.sb([128, 4], F32) for _ in range(2)]
    hbs = [ph.sb([128, D_MODEL], BF16) for _ in range(2)]
    hTs = [ph.sb([128, 8, 512], BF16) for _ in range(2)]
    hTq = [[ph.alias(hTs[s_]) for _ in range(4)] for s_ in range(2)]
    qk = [ph.sb([128, 10, 64], F32) for _ in range(2)]
    sq2 = ph.sb([128, 10, 64], F32)
    ssq = [ph.sb([128, 10], F32) for _ in range(2)]
    rt = [ph.sb([128, 10, 2, 16], F32) for _ in range(4)]
    qkr = [ph.sb([128, 10, 2, 2, 16], BF16) for _ in range(4)]
    qTst = [ph.sb([128, 5, 512], BF16) for _ in range(2)]
    vst = [ph.sb([128, 4, 128], BF16) for _ in range(2)]
    vstq = [[ph.alias(vst[s_]) for _ in range(4)] for s_ in range(2)]
    gast = [ph.sb([128, 4, 512], BF16) for _ in range(2)]
    gastq = [[ph.alias(gast[s_]) for _ in range(4)] for s_ in range(2)]
    ust = [ph.sb([128, 512], BF16) for _ in range(4)]
    ghst = [ph.sb([128, 512], BF16) for _ in range(2)]
    pT = [ph.ps([128, 8, 128], BF16) for _ in range(2)]
    pq = ph.ps([128, 512], F32)
    pkv = ph.ps([128, 512], F32)
    pga = ph.ps([128, 512], F32)
    pf = [ph.ps([128, 512], F32) for _ in range(2)]
    pqT = ph.ps([128, 8, 128], BF16)
    NG = L // 512
    ucnt = 0
    gcnt = 0
    for g in range(NG):
        s_ = g % 2
        xs = xsets[s_]
        for i in range(4):
            r0 = (g * 4 + i) * 128
            DMA("sp", xs[i][:], x_src[r0:r0 + 128, :], w=[xs[i]])
        OP("pool", lambda e, s_=s_: e.memset(ssx[s_][:], 0.0), w=[ssx[s_]])
        for i in range(4):
            OP("act", lambda e, i=i, xs=xs, s_=s_: e.activation(out=sqscr[:], in_=xs[i][:], func=AF.Square,
                                                               accum_out=ssx[s_][:, i:i + 1]), r=[xs[i], ssx[s_]], w=[sqscr, ssx[s_]])
        OP("dve", lambda e, s_=s_: e.tensor_scalar(out=rs[s_][:], in0=ssx[s_][:], scalar1=1.0 / D_MODEL, scalar2=EPS,
                                                  op0=ALU.mult, op1=ALU.add), r=[ssx[s_]], w=[rs[s_]])
        OP("act", lambda e, s_=s_: e.activation(out=rs[s_][:], in_=rs[s_][:], func=AF.Sqrt), r=[rs[s_]], w=[rs[s_]])
        OP("dve", lambda e, s_=s_: e.reciprocal(out=rs[s_][:], in_=rs[s_][:]), r=[rs[s_]], w=[rs[s_]])
        hT = hTs[s_]
        for i in range(4):
            hb = hbs[i % 2]
            OP("dve", lambda e, i=i, hb=hb, xs=xs, s_=s_: e.tensor_scalar(out=hb[:], in0=xs[i][:], scalar1=rs[s_][:, i:i + 1],
                                                                        scalar2=None, op0=ALU.mult), r=[xs[i], rs[s_]], w=[hb])
            p = pT[i % 2]
            for dc in range(8):
                OP("pe", lambda e, dc=dc, hb=hb, p=p: e.transpose(out=p[:, dc, :], in_=hb[:, dc * 128:(dc + 1) * 128],
                                                                identity=identb[:]), r=[hb, identb], w=[p])
            OP("act", lambda e, i=i, p=p, hT=hT: e.activation(out=hT[:, :, i * 128:(i + 1) * 128], in_=p[:], func=AF.Copy),
               r=[p], w=[hTq[s_][i]])
        for i in range(4):
            tix = g * 4 + i
            for (c0, cw, pb) in ((0, 512, pq), (512, 256, pkv), (768, 512, pga)):
                for dc in range(8):
                    OP("pe", lambda e, dc=dc, i=i, c0=c0, cw=cw, pb=pb, hT=hT: e.matmul(
                        pb[:, 0:cw], lhsT=hT[:, dc, i * 128:(i + 1) * 128], rhs=wbf[:, dc, c0:c0 + cw],
                        start=(dc == 0), stop=(dc == 7)), r=[hTq[s_][i], wb[dc]], w=[pb])
            q_ = qk[i % 2]
            OP("act", lambda e, q_=q_: e.activation(out=q_[:, 0:8, :], in_=pq[:].rearrange("p (h d) -> p h d", h=8), func=AF.Copy),
               r=[pq], w=[q_])
            OP("act", lambda e, q_=q_: e.activation(out=q_[:, 8:10, :], in_=pkv[:, 0:128].rearrange("p (h d) -> p h d", h=2),
                                                   func=AF.Copy), r=[pkv], w=[q_])
            OP("act", lambda e, i=i, s_=s_: e.activation(out=vst[s_][:, i, :], in_=pkv[:, 128:256], func=AF.Copy), r=[pkv], w=[vstq[s_][i]])
            OP("act", lambda e, i=i, s_=s_: e.activation(out=gast[s_][:, i, :], in_=pga[:], func=AF.Silu), r=[pga], w=[gastq[s_][i]])
            sq_ = ssq[i % 2]
            OP("dve", lambda e, q_=q_: e.tensor_tensor(out=sq2[:], in0=q_[:], in1=q_[:], op=ALU.mult), r=[q_], w=[sq2])
            OP("dve", lambda e, sq_=sq_: e.tensor_reduce(out=sq_[:], in_=sq2[:], axis=AX.X, op=ALU.add), r=[sq2], w=[sq_])
            OP("dve", lambda e, sq_=sq_: e.tensor_scalar(out=sq_[:], in0=sq_[:], scalar1=1.0 / 64, scalar2=EPS, op0=ALU.mult, op1=ALU.add),
               r=[sq_], w=[sq_])
            OP("act", lambda e, sq_=sq_: e.activation(out=sq_[:], in_=sq_[:], func=AF.Sqrt), r=[sq_], w=[sq_])
            OP("dve", lambda e, sq_=sq_: e.reciprocal(out=sq_[:], in_=sq_[:]), r=[sq_], w=[sq_])
            OP("dve", lambda e, q_=q_, sq_=sq_: e.tensor_tensor(out=q_[:], in0=q_[:], in1=sq_[:].unsqueeze(2).to_broadcast([128, 10, 64]),
                                                              op=ALU.mult), r=[q_, sq_], w=[q_])
            OP("pool", lambda e, q_=q_: e.tensor_tensor(out=q_[:], in0=q_[:], in1=wqk[:], op=ALU.mult), r=[q_, wqk], w=[q_])
            qv = q_[:].rearrange("p h (a m f) -> p h a m f", a=2, m=2)
            x1 = qv[:, :, :, 0, :]
            x2 = qv[:, :, :, 1, :]
            cb_ = cosT[:, tix, :].rearrange("p (a f) -> p a f", a=2).unsqueeze(1).to_broadcast([128, 10, 2, 16])
            sb_ = sinT[:, tix, :].rearrange("p (a f) -> p a f", a=2).unsqueeze(1).to_broadcast([128, 10, 2, 16])
            qo = qkr[(g * 4 + i) % 4]
            t1, t2, t3, t4 = rt
            OP("pool", lambda e, x1=x1, cb_=cb_: e.tensor_tensor(out=t1[:], in0=x1, in1=cb_, op=ALU.mult), r=[q_, cosT], w=[t1])
            OP("pool", lambda e, x2=x2, sb_=sb_: e.tensor_tensor(out=t2[:], in0=x2, in1=sb_, op=ALU.mult), r=[q_, sinT], w=[t2])
            OP("pool", lambda e, qo=qo: e.tensor_tensor(out=qo[:, :, :, 0, :], in0=t1[:], in1=t2[:], op=ALU.subtract), r=[t1, t2], w=[qo])
            OP("dve", lambda e, x2=x2, cb_=cb_: e.tensor_tensor(out=t3[:], in0=x2, in1=cb_, op=ALU.mult), r=[q_, cosT], w=[t3])
            OP("dve", lambda e, x1=x1, sb_=sb_: e.tensor_tensor(out=t4[:], in0=x1, in1=sb_, op=ALU.mult), r=[q_, sinT], w=[t4])
            OP("dve", lambda e, qo=qo: e.tensor_tensor(out=qo[:, :, :, 1, :], in0=t3[:], in1=t4[:], op=ALU.add), r=[t3, t4], w=[qo])
        for fc in range(16):
            p = pf[fc % 2]
            c0 = 1280 + fc * 128
            for dc in range(8):
                OP("pe", lambda e, dc=dc, c0=c0, p=p, hT=hT: e.matmul(p[:], lhsT=wbf[:, dc, c0:c0 + 128], rhs=hT[:, dc, :],
                                                                     start=(dc == 0), stop=(dc == 7)),
                   r=[wb[dc]] + hTq[s_], w=[p])
            if fc < 12:
                u = ust[ucnt % 4]
                ucnt += 1
                OP("dve", lambda e, u=u, p=p: e.tensor_copy(out=u[:], in_=p[:]), r=[p], w=[u])
                DMA("pool", sd.uT[fc * 128:(fc + 1) * 128, g * 512:(g + 1) * 512], u[:], r=[u])
            else:
                gh = ghst[gcnt % 2]
                gcnt += 1
                OP("act", lambda e, gh=gh, p=p: e.activation(out=gh[:], in_=p[:], func=AF.Silu), r=[p], w=[gh])
                DMA("pool", sd.ghT[(fc - 12) * 128:(fc - 11) * 128, g * 512:(g + 1) * 512], gh[:], r=[gh])
        for i in range(4):
            qo = qkr[(g * 4 + i) % 4]
            qf = qo[:].rearrange("p h a m f -> p (h a m f)")
            for c in range(5):
                OP("pe", lambda e, c=c, qf=qf: e.transpose(out=pqT[:, c, :], in_=qf[:, c * 128:(c + 1) * 128], identity=identb[:]),
                   r=[qo, identb], w=[pqT])
            OP("dve", lambda e, i=i, s_=s_: e.tensor_copy(out=qTst[s_][:, :, i * 128:(i + 1) * 128], in_=pqT[:, 0:5, :]), r=[pqT], w=[qTst[s_]])
        DMA("sp", sd.qT[:, :, g * 512:(g + 1) * 512].rearrange("c p t -> p c t"), qTst[s_][:], r=[qTst[s_]])
        DMA("sp", sd.vtok[g * 512:(g + 1) * 512, :].rearrange("(i p) f -> p i f", p=128), vst[s_][:], r=vstq[s_])
        DMA("sp", sd.ga[g * 512:(g + 1) * 512, :].rearrange("(i p) f -> p i f", p=128), gast[s_][:], r=gastq[s_])
    ph.end()


def phase_attn(env, sd, l):
    nc, S = env.nc, env.S
    OP, DMA = mk_ops(env)
    L, NT = sd.L, sd.NT
    ph = Phase(env, "p2")
    kT = ph.sb([128, L], BF16)
    DMA("sp", kT[:], sd.qT[4, :, :], w=[kT])
    vx = ph.sb([128, NT, 2, 128], BF16)
    OP("pool", lambda e: e.memset(vx[:], 1.0), w=[vx])
    for h_ in range(2):
        DMA("sp", vx[:, :, h_, 0:64], sd.vtok.rearrange("(t p) (h d) -> p t h d", p=128, h=2)[:, :, h_, :], w=[vx])
    awn = ph.sb([128, 512], F32)
    DMA("sp", awn[:], env.attn_out_norm_w[l:l + 1, :].partition_broadcast(128), w=[awn])
    identb, identf = env.identb, env.identf
    qts = [ph.sb([128, 4, 2, 512], BF16) for _ in range(2)]
    qtB = [ph.alias(q_) for q_ in qts]
    for q_ in qts:
        OP("pool", lambda e, q_=q_: e.memset(q_[64:128, :, 0, :], 0.0), w=[q_])
        OP("pool", lambda e, q_=q_: e.memset(q_[0:64, :, 1, :], 0.0), w=[q_])
    gats = [ph.sb([128, 4, 512], BF16) for _ in range(2)]
    otoks = [ph.sb([128, 4, 8, 65], F32) for _ in range(2)]
    pts = [ph.sb([128, 1024], BF16) for _ in range(3)]
    osbs = [ph.sb([128, 512], F32) for _ in range(2)]
    rden = ph.sb([128, 8], F32)
    of = ph.sb([128, 8, 64], F32)
    sqs = ph.sb([128, 512], F32)
    ssa = ph.sb([128, 4], F32)
    o2 = ph.sb([128, 512], F32)
    ob = [ph.sb([128, 512], BF16) for _ in range(2)]
    oast = [ph.sb([128, 4, 512], BF16) for _ in range(2)]
    psS = [ph.ps([128, 1024], F32) for _ in range(3)]
    psO = [ph.ps([128, 512], F32) for _ in range(1)]
    psE_full = ph.ps([128, 512], F32)
    psE = T(psE_full.t[:, 0:130].rearrange("p (q d) -> p q d", q=2), psE_full.b)
    psT = T(psE_full.t[:, 256:512].bitcast(BF16).rearrange("p (c t) -> p c t", c=4), ph.env.S.buf())
    NQ = L // 512
    NKG = NT // 2
    cnt = 0
    hcnt = 0
    cg = conv_gen(env, sd, l, ph)
    n_conv_yields = 2 * 12 * (L // min(L, 2048))
    conv_every = max(1, (NQ * 8 * NKG) // (n_conv_yields + 2))
    for qc in range(NQ):
        qt = qts[qc % 2]
        gat = gats[qc % 2]
        otok = otoks[qc % 2]
        qtb = qtB[qc % 2]
        DMA("sp", qt[0:64, :, 0, :], sd.qT[0:4, 0:64, qc * 512:(qc + 1) * 512].rearrange("c p t -> p c t"), w=[qt])
        DMA("sp", qt[64:128, :, 1, :], sd.qT[0:4, 64:128, qc * 512:(qc + 1) * 512].rearrange("c p t -> p c t"), w=[qtb])
        DMA("sp", gat[:], sd.ga[qc * 512:(qc + 1) * 512, :].rearrange("(i p) f -> p i f", p=128), w=[gat])
        steps = [(j, hh, kg) for j in range(4) for hh in range(2) for kg in range(NKG)]

        def QK(st, pS, qt=qt, qtb=qtb):
            j, hh, kg = st
            for u in range(2):
                kb = kg * 2 + u
                OP("pe", lambda e, u=u, kb=kb, hh=hh, j=j, pS=pS, qt=qt: e.matmul(
                    pS[:, u * 512:(u + 1) * 512], lhsT=kT[:, kb * 128:(kb + 1) * 128], rhs=qt[:, j, hh, :],
                    start=True, stop=True), r=[kT, qt, qtb], w=[pS])

        pending = []
        QK(steps[0], psS[cnt % 3])
        if len(steps) > 1:
            QK(steps[1], psS[(cnt + 1) % 3])
        for si, st in enumerate(steps):
            j, hh, kg = st
            pS = psS[cnt % 3]
            pt = pts[cnt % 3]
            if si + 2 < len(steps):
                QK(steps[si + 2], psS[(cnt + 2) % 3])
            OP("act", lambda e, pS=pS, pt=pt: e.activation(out=pt[:], in_=pS[:], func=AF.Exp), r=[pS], w=[pt])
            po = psO[0]
            for u in range(2):
                kb = kg * 2 + u
                OP("pe", lambda e, u=u, kb=kb, hh=hh, pt=pt, po=po: e.matmul(
                    po[:, :], lhsT=vx[:, kb, hh, :], rhs=pt[:, u * 512:(u + 1) * 512],
                    start=(kb == 0), stop=(kb == NT - 1)), r=[vx, pt], w=[po])
            cnt += 1
            if cnt % conv_every == 0:
                next(cg, None)
            if kg == 1 or NKG == 1:
                for fn in pending:
                    fn()
                pending = []
            if kg == NKG - 1:
                head = j + 4 * hh
                osb = osbs[hcnt % 2]
                OP("dve", lambda e, osb=osb, po=po: e.tensor_copy(out=osb[0:65, :], in_=po[0:65, :]), r=[po], w=[osb])

                def epi(osb=osb, head=head, otok=otok):
                    for rr in range(2):
                        for q2 in range(2):
                            qi = rr * 2 + q2
                            OP("pe", lambda e, qi=qi, q2=q2: e.transpose(out=psE[:, q2, :], in_=osb[0:65, qi * 128:(qi + 1) * 128],
                                                                        identity=identf[0:65, 0:65]), r=[osb, identf], w=[psE])
                        OP("dve", lambda e, rr=rr: e.tensor_copy(out=otok[:, rr * 2:rr * 2 + 2, head, :], in_=psE[:]), r=[psE], w=[otok])
                pending.append(epi)
                hcnt += 1
        for fn in pending:
            fn()
        OP("pool", lambda e: e.memset(ssa[:], 0.0), w=[ssa])
        oa = oast[qc % 2]
        for qi in range(4):
            OP("dve", lambda e, qi=qi, otok=otok: e.reciprocal(out=rden[:], in_=otok[:, qi, :, 64]), r=[otok], w=[rden])
            OP("dve", lambda e, qi=qi, otok=otok: e.tensor_tensor(out=of[:], in0=otok[:, qi, :, 0:64],
                                                                 in1=rden[:].unsqueeze(2).to_broadcast([128, 8, 64]), op=ALU.mult),
               r=[otok, rden], w=[of])
            off = of[:].rearrange("p h d -> p (h d)")
            OP("act", lambda e, qi=qi, off=off: e.activation(out=sqs[:], in_=off, func=AF.Square, accum_out=ssa[:, qi:qi + 1]),
               r=[of, ssa], w=[sqs, ssa])
            OP("dve", lambda e, qi=qi: e.tensor_scalar(out=ssa[:, qi:qi + 1], in0=ssa[:, qi:qi + 1], scalar1=1.0 / 512, scalar2=EPS,
                                                      op0=ALU.mult, op1=ALU.add), r=[ssa], w=[ssa])
            OP("act", lambda e, qi=qi: e.activation(out=ssa[:, qi:qi + 1], in_=ssa[:, qi:qi + 1], func=AF.Sqrt), r=[ssa], w=[ssa])
            OP("dve", lambda e, qi=qi: e.reciprocal(out=ssa[:, qi:qi + 1], in_=ssa[:, qi:qi + 1]), r=[ssa], w=[ssa])
            OP("dve", lambda e, qi=qi, off=off: e.scalar_tensor_tensor(out=o2[:], in0=off, scalar=ssa[:, qi:qi + 1], in1=awn[:],
                                                                      op0=ALU.mult, op1=ALU.mult), r=[of, ssa, awn], w=[o2])
            obt = ob[qi % 2]
            OP("dve", lambda e, qi=qi, obt=obt, gat=gat: e.tensor_tensor(out=obt[:], in0=o2[:], in1=gat[:, qi, :], op=ALU.mult),
               r=[o2, gat], w=[obt])
            for c in range(4):
                OP("pe", lambda e, c=c, obt=obt: e.transpose(out=psT[:, c, :], in_=obt[:, c * 128:(c + 1) * 128], identity=identb[:]),
                   r=[obt, identb], w=[psT])
            OP("act", lambda e, qi=qi, oa=oa: e.activation(out=oa[:, :, qi * 128:(qi + 1) * 128], in_=psT[:, :, :], func=AF.Copy), r=[psT], w=[oa])
        DMA("pool", sd.oT[0:4, :, qc * 512:(qc + 1) * 512].rearrange("c p t -> p c t"), oa[:], r=[oa])
    for _ in cg:
        pass
    ph.end()


def conv_gen(env, sd, l, ph):
    OP, DMA = mk_ops(env)
    L = sd.L
    cw = ph.sb([128, 12, 3], F32)
    cb = ph.sb([128, 12], F32)
    for k_ in range(3):
        DMA("sp", cw[:, :, k_], env.conv_w[l, k_, :].rearrange("(c p) -> p c", p=128), w=[cw])
    DMA("sp", cb[:], env.conv_b[l, :].rearrange("(c p) -> p c", p=128), w=[cb])
    TP = min(L, 2048)
    NP = L // TP
    us = [ph.sb([128, TP + 2], BF16) for _ in range(3)]
    os_ = [ph.sb([128, TP], F32) for _ in range(3)]
    obs = [ph.sb([128, TP], BF16) for _ in range(3)]
    cnt = 0
    for c in range(12):
        for pc in range(NP):
            u = us[cnt % 3]
            o = os_[cnt % 3]
            ob = obs[cnt % 3]
            cnt += 1
            t0 = pc * TP
            lo = max(t0 - 1, 0)
            hi = min(t0 + TP + 1, L)
            if pc == 0:
                OP("dve", lambda e, u=u: e.memset(u[:, 0:1], 0.0), w=[u])
            if pc == NP - 1:
                OP("dve", lambda e, u=u: e.memset(u[:, TP + 1:TP + 2], 0.0), w=[u])
            DMA("sp", u[:, lo - (t0 - 1):hi - (t0 - 1)], sd.uT[c * 128:(c + 1) * 128, lo:hi], w=[u])
            yield
            OP("dve", lambda e, u=u, o=o, c=c: e.tensor_scalar(out=o[:], in0=u[:, 1:TP + 1], scalar1=cw[:, c, 1:2], scalar2=cb[:, c:c + 1],
                                                              op0=ALU.mult, op1=ALU.add), r=[u, cw, cb], w=[o])
            OP("dve", lambda e, u=u, o=o, c=c: e.scalar_tensor_tensor(out=o[:], in0=u[:, 0:TP], scalar=cw[:, c, 0:1], in1=o[:],
                                                                     op0=ALU.mult, op1=ALU.add), r=[u, cw, o], w=[o])
            OP("dve", lambda e, u=u, o=o, ob=ob, c=c: e.scalar_tensor_tensor(out=ob[:], in0=u[:, 2:TP + 2], scalar=cw[:, c, 2:3], in1=o[:],
                                                                            op0=ALU.mult, op1=ALU.add), r=[u, cw, o], w=[ob])
            if c < 4:
                DMA("pool", sd.vbf[c * 128:(c + 1) * 128, t0:t0 + TP], ob[:], r=[ob])
            else:
                DMA("pool", sd.ucT[(c - 4) * 128:(c - 3) * 128, t0:t0 + TP], ob[:], r=[ob])
            yield


def phase_filters(env, sd, l):
    ph = Phase(env, "hf")
    for _ in filters_gen(env, sd, l, ph):
        pass
    ph.end()


def filters_gen(env, sd, l, ph):
    OP, DMA = mk_ops(env)
    L = sd.L
    w1 = ph.sb([33, 64], F32)
    w2 = ph.sb([64, 64], F32)
    w3 = ph.sb([64, 2048], F32)
    w3b = ph.sb([64, 2048], BF16)
    h2b = ph.sb([64, min(L, 2048)], BF16)
    DMA("sp", w1[:], env.filt_w1[l, :, :], w=[w1])
    DMA("sp", w2[:], env.filt_w2[l, :, :], w=[w2])
    DMA("sp", w3[:], env.filt_w3[l, :, :], w=[w3])
    OP("pool", lambda e: e.tensor_copy(out=w3b[:], in_=w3[:]), r=[w3], w=[w3b])
    fr = ph.sb([64, 1], F32)
    b1 = ph.sb([64, 1], F32)
    b2 = ph.sb([64, 1], F32)
    DMA("sp", fr[:], env.filt_freq[l, :].rearrange("(p o) -> p o", o=1), w=[fr])
    DMA("sp", b1[:], env.filt_b1[l, :].rearrange("(p o) -> p o", o=1), w=[b1])
    DMA("sp", b2[:], env.filt_b2[l, :].rearrange("(p o) -> p o", o=1), w=[b2])
    OP("dve", lambda e: e.tensor_tensor(out=b1[:], in0=b1[:], in1=fr[:], op=ALU.mult), r=[b1, fr], w=[b1])
    OP("dve", lambda e: e.tensor_tensor(out=b2[:], in0=b2[:], in1=fr[:], op=ALU.mult), r=[b2, fr], w=[b2])
    nad = ph.sb([128, 16], F32)
    DMA("sp", nad[:], env.filt_decay[l, :].rearrange("(q p) -> p q", p=128), w=[nad])
    nad2 = ph.sb([128, 16], F32)
    OP("dve", lambda e: e.tensor_scalar(out=nad2[:], in0=nad[:], scalar1=-1.0, scalar2=None, op0=ALU.mult), r=[nad], w=[nad2])
    OP("dve", lambda e: e.tensor_tensor(out=nad[:], in0=nad[:], in1=nad2[:], op=ALU.min), r=[nad, nad2], w=[nad])
    TP = min(L, 2048)
    NP = L // TP
    NS = TP // 512
    tv = ph.sb([128, TP], F32)
    zt = ph.sb([33, TP], F32)
    h1 = ph.sb([64, TP], F32)
    h2 = ph.sb([64, TP], F32)
    ta = ph.sb([64, 512], F32)
    tw = ph.sb([64, 512], F32)
    asum = ph.sb([128, 16, NP * NS], F32)
    OP("pool", lambda e: e.memset(asum[:], 0.0), w=[asum])
    wins = [ph.sb([128, 512], F32) for _ in range(2)]
    taps = [ph.sb([128, 512], F32) for _ in range(2)]
    junk = ph.sb([128, 512], F32)
    tbf = [ph.sb([128, 512], BF16) for _ in range(4)]
    pm = [ph.ps([128, 512], F32) for _ in range(2)]
    pt_ = [ph.ps([128, 512], F32) for _ in range(2)]

    def sin_layer(pp, frb, dst_t, sl):
        OP("dve", lambda e: e.tensor_scalar(out=ta[:], in0=pp[0:64, :], scalar1=fr[:, 0:1], scalar2=frb[:, 0:1], op0=ALU.mult, op1=ALU.add),
           r=[pp, fr, frb], w=[ta])
        for _ in range(2):
            OP("dve", lambda e: e.tensor_scalar(out=tw[:], in0=ta[:], scalar1=PI, scalar2=-2 * PI, op0=ALU.is_gt, op1=ALU.mult), r=[ta], w=[tw])
            OP("dve", lambda e: e.tensor_tensor(out=ta[:], in0=ta[:], in1=tw[:], op=ALU.add), r=[ta, tw], w=[ta])
            OP("dve", lambda e: e.tensor_scalar(out=tw[:], in0=ta[:], scalar1=-PI, scalar2=2 * PI, op0=ALU.is_lt, op1=ALU.mult), r=[ta], w=[tw])
            OP("dve", lambda e: e.tensor_tensor(out=ta[:], in0=ta[:], in1=tw[:], op=ALU.add), r=[ta, tw], w=[ta])
        OP("act", lambda e: e.activation(out=dst_t[:, sl], in_=ta[:], func=AF.Sin), r=[ta], w=[dst_t])

    cnt = 0
    deferred = []
    for pc in range(NP):
        t0 = pc * TP
        DMA("sp", zt[:], sd.c["zT"][:, t0:t0 + TP], w=[zt])
        DMA("sp", tv[:], sd.c["tv"][:, t0:t0 + TP].partition_broadcast(128), w=[tv])
        for sblk in range(NS):
            sl = slice(sblk * 512, (sblk + 1) * 512)
            p = pm[0]
            OP("pe", lambda e, p=p, sl=sl: e.matmul(p[0:64, :], lhsT=w1[:, :], rhs=zt[:, sl], start=True, stop=True), r=[w1, zt], w=[p])
            sin_layer(p, b1, h1, sl)
            p = pm[1]
            OP("pe", lambda e, p=p, sl=sl: e.matmul(p[0:64, :], lhsT=w2[:, :], rhs=h1[:, sl], start=True, stop=True), r=[w2, h1], w=[p])
            sin_layer(p, b2, h2, sl)
            OP("pool", lambda e, sl=sl: e.tensor_copy(out=h2b[:, sl], in_=h2[:, sl]), r=[h2], w=[h2b])
            yield
        for q in range(16):
            o_, dr, ch = q // 8, (q // 4) % 2, q % 4
            for sblk in range(NS):
                sl = slice(sblk * 512, (sblk + 1) * 512)
                p = pt_[cnt % 2]
                win = wins[cnt % 2]
                tap = taps[cnt % 2]
                tb = tbf[cnt % 4]
                cnt += 1
                OP("pe", lambda e, p=p, q=q, sl=sl: e.matmul(p[:], lhsT=w3b[:, q * 128:(q + 1) * 128], rhs=h2b[:, sl], start=True, stop=True),
                   r=[w3b, h2b], w=[p])
                OP("act", lambda e, win=win, q=q, sl=sl: e.activation(out=win[:], in_=tv[:, sl], func=AF.Exp, scale=nad[:, q:q + 1]),
                   r=[tv, nad], w=[win])
                OP("dve", lambda e, win=win, tb=tb, p=p: e.scalar_tensor_tensor(out=tb[:], in0=win[:], scalar=0.05, in1=p[:],
                                                                               op0=ALU.add, op1=ALU.mult), r=[win, p], w=[tb])
                if dr == 1 and pc == 0 and sblk == 0:
                    OP("dve", lambda e, tb=tb: e.memset(tb[:, 0:1], 0.0), w=[tb])
                idx = pc * NS + sblk
                for fn in deferred:
                    fn()
                deferred.clear()

                def fin(tb=tb, q=q, idx=idx, o_=o_, dr=dr, ch=ch, c0_=t0 + sblk * 512):
                    OP("act", lambda e: e.activation(out=junk[:], in_=tb[:], func=AF.Abs, accum_out=asum[:, q, idx:idx + 1]),
                       r=[tb, asum], w=[junk, asum])
                    DMA("sp", sd.hbf[o_, dr, ch * 128:(ch + 1) * 128, c0_:c0_ + 512], tb[:], r=[tb])
                deferred.append(fin)
                yield
    for fn in deferred:
        fn()
    deferred.clear()
    tot = ph.sb([128, 16], F32)
    OP("dve", lambda e: e.tensor_reduce(out=tot[:], in_=asum[:], axis=AX.X, op=ALU.add), r=[asum], w=[tot])
    tv4 = tot[:].rearrange("p (o d c) -> p o d c", o=2, d=2)
    rn = env.rnorm[l % 2]
    OP("dve", lambda e: e.tensor_tensor(out=rn[:].rearrange("p (o c) -> p o c", o=2), in0=tv4[:, :, 0, :], in1=tv4[:, :, 1, :], op=ALU.add),
       r=[tot], w=[rn])
    OP("dve", lambda e: e.reciprocal(out=rn[:], in_=rn[:]), r=[rn], w=[rn])
    yield


class FFTC:
    pass


def load_fft_consts(env, ph, sd):
    OP, DMA = mk_ops(env)
    NT, K1r, K1, CPB, CB, batches = sd.cfg
    f = FFTC()
    f.F1 = ph.sb([NT, 2 * K1], BF16)
    f.twr = ph.sb([128, K1], BF16)
    f.twi = ph.sb([128, K1], BF16)
    f.c2 = ph.sb([128, 4, 128], BF16)
    f.e12 = ph.sb([128, 3, 256], BF16)
    f.tir = ph.sb([K1, 128], BF16)
    f.tii = ph.sb([K1, 128], BF16)
    f.c1w = ph.sb([128, 3, NT], BF16)
    for nm in ("F1", "twr", "twi", "c2", "e12", "tir", "tii", "c1w"):
        t = getattr(f, nm)
        src = sd.c[nm]
        DMA("sp", t[:], src, w=[t])
    return f


def cmul4(env, src_re, src_im, tab_a, tab_b, outs, r, w):
    OP, DMA = mk_ops(env)
    for i_, (x, tb, o) in enumerate(((src_re, tab_a, outs[0]), (src_im, tab_b, outs[1]), (src_re, tab_b, outs[2]), (src_im, tab_a, outs[3]))):
        eng = "pool" if i_ == 3 else "dve"
        OP(eng, lambda e, x=x, tb=tb, o=o: e.tensor_tensor(out=o, in0=x, in1=tb, op=ALU.mult), r=r, w=w)


def fft_stage1_tw(env, sd, f, X, ncols, TT, psA, cntA, EV):
    OP, DMA = mk_ops(env)
    NT, K1r, K1, CPB, CB, batches = sd.cfg
    W = CPB * 2 * K1
    c = 0
    while c < ncols:
        ng = min(2 * CPB, ncols - c)
        pa = psA[cntA % len(psA)]
        ev = EV[cntA % len(EV)]
        cntA += 1
        for s_ in range(ng):
            b_, o_ = divmod(s_, CPB)
            OP("pe", lambda e, s_=s_, c=c, pa=pa, b_=b_, o_=o_: e.matmul(pa[:, b_, o_ * 2 * K1:(o_ + 1) * 2 * K1], lhsT=X[0:NT, c + s_, :],
                                                                         rhs=f.F1[:, :], start=True, stop=True), r=[X, f.F1], w=[pa])
        if ng == 2 * CPB:
            OP("act", lambda e, pa=pa, ev=ev: e.activation(out=ev[:, :, 0:W], in_=pa[:, :, 0:W], func=AF.Copy), r=[pa], w=[ev])
            pv = ev[:, :, 0:W].rearrange("p b (c r k) -> p b c r k", c=CPB, r=2)
            are, aim = pv[:, :, :, 0, :], pv[:, :, :, 1, :]
            twr = f.twr[:].unsqueeze(1).unsqueeze(1).to_broadcast([128, 2, CPB, K1])
            twi = f.twi[:].unsqueeze(1).unsqueeze(1).to_broadcast([128, 2, CPB, K1])
            outs = [t[:, c * K1:(c + ng) * K1].rearrange("p (b c k) -> p b c k", b=2, c=CPB) for t in TT]
            cmul4(env, are, aim, twr, twi, outs, r=[ev, f.twr, f.twi], w=TT)
        else:
            done = 0
            while done < ng:
                nb = min(CPB, ng - done)
                b_ = done // CPB
                OP("act", lambda e, pa=pa, ev=ev, b_=b_, nb=nb: e.activation(out=ev[:, b_, 0:nb * 2 * K1], in_=pa[:, b_, 0:nb * 2 * K1], func=AF.Copy),
                   r=[pa], w=[ev])
                pv = ev[:, b_, 0:nb * 2 * K1].rearrange("p (c r k) -> p c r k", c=nb, r=2)
                are, aim = pv[:, :, 0, :], pv[:, :, 1, :]
                twr = f.twr[:].unsqueeze(1).to_broadcast([128, nb, K1])
                twi = f.twi[:].unsqueeze(1).to_broadcast([128, nb, K1])
                outs = [t[:, (c + done) * K1:(c + done + nb) * K1].rearrange("p (c k) -> p c k", c=nb) for t in TT]
                cmul4(env, are, aim, twr, twi, outs, r=[ev, f.twr, f.twi], w=TT)
                done += nb
        c += ng
    return cntA


SEQ_RE = ((0, 0), (3, 1), (1, 2), (1, 3))
SEQ_IM = ((0, 2), (0, 3), (2, 0), (1, 1))
SEQ_IM_NEG = ((3, 2), (3, 3), (1, 0), (2, 1))


def phase_filter_fft(env, sd, l):
    OP, DMA = mk_ops(env)
    L = sd.L
    NT, K1r, K1, CPB, CB, batches = sd.cfg
    ph = Phase(env, "hff")
    f = load_fft_consts(env, ph, sd)
    Xs = [[ph.sb([NT, CB, 128], BF16) for _ in range(2)] for _ in range(2)]
    TTs = [[[ph.sb([128, CB * K1], BF16) for _ in range(4)] for _ in range(2)] for _ in range(2)]
    Hs = [ph.sb([128, 2, CB * K1], BF16) for _ in range(2)]
    psA = [ph.ps([128, 2, 512], F32) for _ in range(2)]
    psB = [ph.ps([128, 2, 512], F32) for _ in range(2)]
    EV = [ph.sb([128, 2, 512], BF16) for _ in range(3)]
    st = {"cntA": 0, "cntB": 0}
    work = [(o_, bidx, c0, ncols) for o_ in range(2) for bidx, (c0, ncols) in enumerate(batches)]

    def load_x(i):
        o_, bidx, c0, ncols = work[i]
        Xp = Xs[i % 2]
        for dr in range(2):
            DMA("sp", Xp[dr][:, 0:ncols, :], sd.hbf[o_, dr, c0:c0 + ncols, :].rearrange("c (a b) -> a c b", b=128), w=[Xp[dr]])

    def stage_a(i):
        o_, bidx, c0, ncols = work[i]
        Xp = Xs[i % 2]
        for dr in range(2):
            st["cntA"] = fft_stage1_tw(env, sd, f, Xp[dr], ncols, TTs[i % 2][dr], psA, st["cntA"], EV)

    def stage_b(i):
        o_, bidx, c0, ncols = work[i]
        Hsb = Hs[i % 2]
        TTp = TTs[i % 2]
        tot = ncols * K1
        g0 = 0
        while g0 < tot:
            gw = min(512, tot - g0)
            pb = psB[st["cntB"] % 2]
            st["cntB"] += 1
            sl = slice(g0, g0 + gw)
            seq_re = [(mi, TTp[0][ti]) for mi, ti in SEQ_RE] + [(mi, TTp[1][ti]) for mi, ti in SEQ_RE]
            seq_im = [(mi, TTp[0][ti]) for mi, ti in SEQ_IM] + [(mi, TTp[1][ti]) for mi, ti in SEQ_IM_NEG]
            for ri, seq in enumerate((seq_re, seq_im)):
                for k, (mi, Bt) in enumerate(seq):
                    OP("pe", lambda e, ri=ri, mi=mi, Bt=Bt, k=k, pb=pb, sl=sl, gw=gw: e.matmul(
                        pb[:, ri, 0:gw], lhsT=f.c2[:, mi, :], rhs=Bt[:, sl], start=(k == 0), stop=(k == 7)), r=[f.c2, Bt], w=[pb])
            OP("dve", lambda e, pb=pb, sl=sl, gw=gw, Hsb=Hsb: e.tensor_copy(out=Hsb[:, :, sl], in_=pb[:, :, 0:gw]), r=[pb], w=[Hsb])
            g0 += gw
        DMA("sp", sd.Hd[o_, bidx, :, :, 0:tot].rearrange("r p x -> p r x"), Hsb[:, :, 0:tot], r=[Hsb])

    n = len(work)
    load_x(0)
    for t in range(n + 1):
        if t < n:
            stage_a(t)
        if t + 1 < n:
            load_x(t + 1)
        if t >= 1:
            stage_b(t - 1)
    ph.end()


def phase_fftconv(env, sd, l, o_):
    OP, DMA = mk_ops(env)
    L = sd.L
    NT, K1r, K1, CPB, CB, batches = sd.cfg
    src = sd.vbf if o_ == 0 else sd.zbf
    ph = Phase(env, "hc")
    f = load_fft_consts(env, ph, sd)
    Xs = [ph.sb([NT, CB, 128], BF16) for _ in range(2)]
    Hs = [ph.sb([128, 2, CB * K1], BF16) for _ in range(2)]
    TTs = [[ph.sb([128, CB * K1], BF16) for _ in range(4)] for _ in range(2)]
    UUs = [[ph.sb([128, CB * K1 + 128], BF16) for _ in range(4)] for _ in range(2)]
    VVs = [[ph.sb([128, CB * 128], BF16) for _ in range(4)] for _ in range(2)]
    for uu_ in UUs:
        for u_ in uu_:
            OP("pool", lambda e, u_=u_: e.memset(u_[:, CB * K1:CB * K1 + 128], 0.0), w=[u_])
    for vv_ in VVs:
        for v_ in vv_:
            OP("pool", lambda e, v_=v_: e.memset(v_[:], 0.0), w=[v_])
    ysb = [ph.sb([NT, CB * 128], F32) for _ in range(2)]
    psA = [ph.ps([128, 2, 512], F32) for _ in range(2)]
    psB = [ph.ps([128, 2, 512], F32) for _ in range(2)]
    EV = [ph.sb([128, 2, 512], BF16) for _ in range(4)]
    st = {"cntA": 0, "cntB": 0, "cntE": 0}

    def load_x(bi):
        c0, ncols = batches[bi]
        X = Xs[bi % 2]
        DMA("sp", X[:, 0:ncols, :], src[c0:c0 + ncols, :].rearrange("c (a b) -> a c b", b=128), w=[X])

    def load_h(bi):
        c0, ncols = batches[bi]
        H = Hs[bi % 2]
        DMA("sp", H[:, :, 0:ncols * K1], sd.Hd[o_, bi, :, :, 0:ncols * K1].rearrange("r p x -> p r x"), w=[H])

    def stage_a(bi):
        c0, ncols = batches[bi]
        X = Xs[bi % 2]
        st["cntA"] = fft_stage1_tw(env, sd, f, X, ncols, TTs[bi % 2], psA, st["cntA"], EV)

    def stage_b(bi):
        c0, ncols = batches[bi]
        H = Hs[bi % 2]
        TT = TTs[bi % 2]
        UU = UUs[bi % 2]
        tot = ncols * K1
        g0 = 0
        while g0 < tot:
            gw = min(512, tot - g0)
            pb = psB[st["cntB"] % 2]
            st["cntB"] += 1
            sl = slice(g0, g0 + gw)
            for ri, seq in enumerate((SEQ_RE, SEQ_IM)):
                for k, (mi, ti) in enumerate(seq):
                    Bt = TT[ti]
                    OP("pe", lambda e, ri=ri, mi=mi, Bt=Bt, k=k, pb=pb, sl=sl, gw=gw: e.matmul(
                        pb[:, ri, 0:gw], lhsT=f.c2[:, mi, :], rhs=Bt[:, sl], start=(k == 0), stop=(k == 3)), r=[f.c2, Bt], w=[pb])
            ev = EV[st["cntE"] % 4]
            st["cntE"] += 1
            OP("act", lambda e, pb=pb, ev=ev, gw=gw: e.activation(out=ev[:, :, 0:gw], in_=pb[:, :, 0:gw], func=AF.Copy), r=[pb], w=[ev])
            cmul4(env, ev[:, 0, 0:gw], ev[:, 1, 0:gw], H[:, 0, sl], H[:, 1, sl], [u[:, sl] for u in UU], r=[ev, H], w=UU)
            g0 += gw

    def stage_c(bi):
        c0, ncols = batches[bi]
        UU = UUs[bi % 2]
        VV = VVs[bi % 2]
        c = 0
        while c < ncols:
            ng = min(4, ncols - c)
            pa = psA[st["cntA"] % 2]
            st["cntA"] += 1
            for s_ in range(ng):
                col = c + s_
                b_, o2 = divmod(s_, 2)
                for k, (ui, ei) in enumerate(((0, 0), (1, 2), (2, 1), (3, 1))):
                    U = UU[ui]
                    OP("pe", lambda e, b_=b_, o2=o2, col=col, pa=pa, U=U, ei=ei, k=k: e.matmul(
                        pa[:, b_, o2 * 256:(o2 + 1) * 256], lhsT=U[:, col * K1:col * K1 + 128], rhs=f.e12[:, ei, :],
                        start=(k == 0), stop=(k == 3)), r=[U, f.e12], w=[pa])
            ev = EV[st["cntE"] % 4]
            st["cntE"] += 1
            if ng == 4:
                OP("act", lambda e, pa=pa, ev=ev: e.activation(out=ev[0:K1, :, :], in_=pa[0:K1, :, :], func=AF.Copy), r=[pa], w=[ev])
                pv = ev[0:K1, :, :].rearrange("p b (c r n) -> p b c r n", c=2, r=2)
                zre, zim = pv[:, :, :, 0, :], pv[:, :, :, 1, :]
                tir = f.tir[:].unsqueeze(1).unsqueeze(1).to_broadcast([K1, 2, 2, 128])
                tii = f.tii[:].unsqueeze(1).unsqueeze(1).to_broadcast([K1, 2, 2, 128])
                outs = [v[0:K1, c * 128:(c + 4) * 128].rearrange("p (b c n) -> p b c n", b=2, c=2) for v in VV]
                cmul4(env, zre, zim, tir, tii, outs, r=[ev, f.tir, f.tii], w=VV)
            else:
                done = 0
                while done < ng:
                    nb = min(2, ng - done)
                    b_ = done // 2
                    OP("act", lambda e, pa=pa, ev=ev, b_=b_, nb=nb: e.activation(out=ev[0:K1, b_, 0:nb * 256], in_=pa[0:K1, b_, 0:nb * 256], func=AF.Copy),
                       r=[pa], w=[ev])
                    pv = ev[0:K1, b_, 0:nb * 256].rearrange("p (c r n) -> p c r n", c=nb, r=2)
                    zre, zim = pv[:, :, 0, :], pv[:, :, 1, :]
                    tir = f.tir[:].unsqueeze(1).to_broadcast([K1, nb, 128])
                    tii = f.tii[:].unsqueeze(1).to_broadcast([K1, nb, 128])
                    outs = [v[0:K1, (c + done) * 128:(c + done + nb) * 128].rearrange("p (c n) -> p c n", c=nb) for v in VV]
                    cmul4(env, zre, zim, tir, tii, outs, r=[ev, f.tir, f.tii], w=VV)
                    done += nb
            c += ng

    def stage_d(bi):
        c0, ncols = batches[bi]
        VV = VVs[bi % 2]
        ys = ysb[bi % 2]
        tot2 = ncols * 128
        g0 = 0
        while g0 < tot2:
            gw = min(512, tot2 - g0)
            pb = psB[st["cntB"] % 2]
            st["cntB"] += 1
            sl = slice(g0, g0 + gw)
            for k, (vi, wi) in enumerate(((0, 0), (1, 2), (2, 1), (3, 1))):
                V = VV[vi]
                OP("pe", lambda e, pb=pb, sl=sl, gw=gw, V=V, wi=wi, k=k: e.matmul(pb[0:NT, 0, 0:gw], lhsT=f.c1w[:, wi, :], rhs=V[:, sl],
                                                                              start=(k == 0), stop=(k == 3)), r=[f.c1w, V], w=[pb])
            OP("dve", lambda e, pb=pb, sl=sl, gw=gw, ys=ys: e.tensor_copy(out=ys[:, sl], in_=pb[0:NT, 0, 0:gw]), r=[pb], w=[ys])
            g0 += gw
        DMA("sp", sd.ybuf[c0:c0 + ncols, :].rearrange("c (a b) -> a c b", b=128), ys[:, 0:tot2].rearrange("p (c b) -> p c b", b=128), r=[ys])

    n = len(batches)
    load_x(0)
    load_h(0)
    for t in range(n + 3):
        if t < n:
            stage_a(t)
        if t + 1 < n:
            load_x(t + 1)
        if 0 <= t - 1 < n:
            stage_b(t - 1)
        if t + 1 < n:
            load_h(t + 1)
        if 0 <= t - 2 < n:
            stage_c(t - 2)
        if 0 <= t - 3 < n:
            stage_d(t - 3)
    ph.end()


def phase_gate(env, sd, l, o_):
    OP, DMA = mk_ops(env)
    L = sd.L
    ph = Phase(env, "hg")
    hbz = ph.sb([128, 2, 4], F32)
    for o2_ in range(2):
        DMA("sp", hbz[:, o2_, :], env.hyena_bias[l, o2_, :].rearrange("(c p) -> p c", p=128), w=[hbz])
    rn = env.rnorm[l % 2]
    TP = min(L, 1024)
    NP = L // TP
    vsrc = sd.vbf if o_ == 0 else sd.zbf
    if o_ == 0:
        ys = [ph.sb([128, TP], F32) for _ in range(2)]
        vs = [ph.sb([128, TP], BF16) for _ in range(2)]
        xs = [ph.sb([128, TP], BF16) for _ in range(2)]
        zs = [ph.sb([128, TP], BF16) for _ in range(2)]
        cnt = 0
        for c in range(4):
            for pc in range(NP):
                y, v, x, z = ys[cnt % 2], vs[cnt % 2], xs[cnt % 2], zs[cnt % 2]
                eng = "dve"
                cnt += 1
                t0 = pc * TP
                rows = slice(c * 128, (c + 1) * 128)
                DMA("sp", y[:], sd.ybuf[rows, t0:t0 + TP], w=[y])
                DMA("sp", v[:], vsrc[rows, t0:t0 + TP], w=[v])
                DMA("sp", x[:], sd.ucT[rows, t0:t0 + TP], w=[x])
                OP(eng, lambda e, y=y, c=c: e.tensor_scalar(out=y[:], in0=y[:], scalar1=rn[:, c:c + 1], scalar2=None, op0=ALU.mult), r=[y, rn], w=[y])
                OP(eng, lambda e, y=y, v=v, c=c: e.scalar_tensor_tensor(out=y[:], in0=v[:], scalar=hbz[:, 0, c:c + 1], in1=y[:], op0=ALU.mult, op1=ALU.add),
                   r=[y, v, hbz], w=[y])
                OP(eng, lambda e, y=y, x=x, z=z: e.tensor_tensor(out=z[:], in0=x[:], in1=y[:], op=ALU.mult), r=[x, y], w=[z])
                DMA("sp", sd.zbf[rows, t0:t0 + TP], z[:], r=[z])
    else:
        hw = ph.sb([128, 4], F32)
        DMA("sp", hw[:], env.hyena_out_norm_w[l, :].rearrange("(c p) -> p c", p=128), w=[hw])
        ones = env.onesb
        ys = [ph.sb([128, TP], F32) for _ in range(4)]
        vs = [ph.sb([128, TP], BF16) for _ in range(4)]
        xs = [ph.sb([128, TP], BF16) for _ in range(4)]
        ghs = [ph.sb([128, TP], BF16) for _ in range(4)]
        sqb = [ph.sb([128, TP], BF16) for _ in range(4)]
        rsd = ph.sb([128, TP], F32)
        obf = [ph.sb([128, TP], BF16) for _ in range(4)]
        pss = [ph.ps([128, 512], F32) for _ in range(2)]
        for pc in range(NP):
            t0 = pc * TP
            for c in range(4):
                y, v, x, gh = ys[c], vs[c], xs[c], ghs[c]
                rows = slice(c * 128, (c + 1) * 128)
                eng = "dve"
                DMA("sp", y[:], sd.ybuf[rows, t0:t0 + TP], w=[y])
                DMA("sp", v[:], vsrc[rows, t0:t0 + TP], w=[v])
                DMA("sp", x[:], sd.ucT[512 + c * 128:512 + (c + 1) * 128, t0:t0 + TP], w=[x])
                DMA("pool", gh[:], sd.ghT[rows, t0:t0 + TP], w=[gh])
                OP(eng, lambda e, y=y, c=c: e.tensor_scalar(out=y[:], in0=y[:], scalar1=rn[:, 4 + c:5 + c], scalar2=None, op0=ALU.mult), r=[y, rn], w=[y])
                OP(eng, lambda e, y=y, v=v, c=c: e.scalar_tensor_tensor(out=y[:], in0=v[:], scalar=hbz[:, 1, c:c + 1], in1=y[:], op0=ALU.mult, op1=ALU.add),
                   r=[y, v, hbz], w=[y])
                OP(eng, lambda e, y=y, x=x: e.tensor_tensor(out=y[:], in0=x[:], in1=y[:], op=ALU.mult), r=[x, y], w=[y])
                OP(eng, lambda e, y=y, c=c: e.tensor_tensor(out=sqb[c][:], in0=y[:], in1=y[:], op=ALU.mult), r=[y], w=[sqb[c]])
            for sblk in range(TP // 512):
                sl = slice(sblk * 512, (sblk + 1) * 512)
                p = pss[sblk % 2]
                for c in range(4):
                    OP("pe", lambda e, c=c, p=p, sl=sl: e.matmul(p[:], lhsT=ones[:], rhs=sqb[c][:, sl], start=(c == 0), stop=(c == 3)),
                       r=[ones, sqb[c]], w=[p])
                OP("dve", lambda e, p=p, sl=sl: e.tensor_scalar(out=rsd[:, sl], in0=p[:], scalar1=1.0 / 512, scalar2=EPS, op0=ALU.mult, op1=ALU.add),
                   r=[p], w=[rsd])
            OP("act", lambda e: e.activation(out=rsd[:], in_=rsd[:], func=AF.Sqrt), r=[rsd], w=[rsd])
            OP("dve", lambda e: e.reciprocal(out=rsd[:], in_=rsd[:]), r=[rsd], w=[rsd])
            for c in range(4):
                y, gh, ob = ys[c], ghs[c], obf[c]
                eng = "dve"
                OP(eng, lambda e, y=y, c=c: e.scalar_tensor_tensor(out=y[:], in0=y[:], scalar=hw[:, c:c + 1], in1=rsd[:], op0=ALU.mult, op1=ALU.mult),
                   r=[y, hw, rsd], w=[y])
                OP(eng, lambda e, y=y, gh=gh, ob=ob: e.tensor_tensor(out=ob[:], in0=y[:], in1=gh[:], op=ALU.mult), r=[y, gh], w=[ob])
                DMA("sp", sd.oT[4 + c, :, t0:t0 + TP], ob[:], r=[ob])
    ph.end()


def phase_outproj(env, sd, l, last):
    OP, DMA = mk_ops(env)
    L, NT = sd.L, sd.NT
    x_src = sd.x_in if l == 0 else sd.xres
    ph = Phase(env, "p4")
    fg = None if last else filters_gen(env, sd, l + 1, ph)
    TPf = min(L, 2048)
    n_fy = (L // TPf) * ((TPf // 512) * 17) + 1
    f_per_tile = -(-n_fy // (L // 128)) if fg is not None else 0
    wob = ph.sb([128, 8, D_MODEL], BF16)
    wbb = [ph.alias(wob) for _ in range(8)]
    for dc in range(8):
        DMA("sp" if dc % 2 == 0 else "act", wob[:, dc, :], env.woutb[l, :, dc, :], w=[wbb[dc]])
    fw = None
    if last:
        fw = ph.sb([128, D_MODEL], F32)
        DMA("sp", fw[:], env.final_norm_w.rearrange("(o d) -> o d", o=1).partition_broadcast(128), w=[fw])
    oTs = [ph.sb([128, 8, 512], BF16) for _ in range(2)]
    xts = [ph.sb([128, D_MODEL], F32) for _ in range(3)]
    xns = [ph.sb([128, D_MODEL], F32) for _ in range(3)]
    sqscr = ph.sb([128, D_MODEL], F32)
    ss = [ph.sb([128, 1], F32) for _ in range(2)]
    pss = [ph.ps([128, 512], F32) for _ in range(4)]
    NG = L // 512
    cnt = 0
    for g in range(NG):
        oTt = oTs[g % 2]
        DMA("sp", oTt[:], sd.oT[:, :, g * 512:(g + 1) * 512].rearrange("c p t -> p c t"), w=[oTt])
        for i in range(4):
            r0 = (g * 4 + i) * 128
            xt = xts[cnt % 3]
            xn = xns[cnt % 3]
            s1 = ss[cnt % 2]
            DMA("pool", xt[:], x_src[r0:r0 + 128, :], w=[xt])
            for hf in range(2):
                p = pss[(cnt * 2 + hf) % 4]
                for c in range(8):
                    OP("pe", lambda e, c=c, i=i, hf=hf, p=p, oTt=oTt: e.matmul(p[:], lhsT=oTt[:, c, i * 128:(i + 1) * 128],
                                                                              rhs=wob[:, c, hf * 512:(hf + 1) * 512], start=(c == 0), stop=(c == 7)),
                       r=[oTt, wbb[c]], w=[p])
                OP("dve", lambda e, hf=hf, p=p, xt=xt, xn=xn: e.tensor_tensor(out=xn[:, hf * 512:(hf + 1) * 512], in0=p[:],
                                                                             in1=xt[:, hf * 512:(hf + 1) * 512], op=ALU.add), r=[p, xt], w=[xn])
            if not last:
                DMA("sp", sd.xres[r0:r0 + 128, :], xn[:], r=[xn])
            else:
                OP("pool", lambda e, s1=s1: e.memset(s1[:], 0.0), w=[s1])
                OP("act", lambda e, xn=xn, s1=s1: e.activation(out=sqscr[:], in_=xn[:], func=AF.Square, accum_out=s1[:, 0:1]), r=[xn, s1], w=[sqscr, s1])
                OP("dve", lambda e, s1=s1: e.tensor_scalar(out=s1[:], in0=s1[:], scalar1=1.0 / D_MODEL, scalar2=EPS, op0=ALU.mult, op1=ALU.add), r=[s1], w=[s1])
                OP("act", lambda e, s1=s1: e.activation(out=s1[:], in_=s1[:], func=AF.Sqrt), r=[s1], w=[s1])
                OP("dve", lambda e, s1=s1: e.reciprocal(out=s1[:], in_=s1[:]), r=[s1], w=[s1])
                OP("dve", lambda e, xn=xn, s1=s1: e.scalar_tensor_tensor(out=xn[:], in0=xn[:], scalar=s1[:, 0:1], in1=fw[:], op0=ALU.mult, op1=ALU.mult),
                   r=[xn, s1, fw], w=[xn])
                DMA("sp", sd.y_out[r0:r0 + 128, :], xn[:], r=[xn])
            cnt += 1
            for _ in range(f_per_tile):
                next(fg, None)
    if fg is not None:
        for _ in fg:
            pass
    ph.end()


WEIGHT_SPECS = [
    ("norm_w", (DEPTH, 1024)), ("w_in", (DEPTH, 1024, DIN)), ("q_norm_w", (DEPTH, 64)), ("k_norm_w", (DEPTH, 64)),
    ("conv_w", (DEPTH, 3, 1536)), ("conv_b", (DEPTH, 1536)), ("filt_w1", (DEPTH, 33, 64)), ("filt_b1", (DEPTH, 64)),
    ("filt_w2", (DEPTH, 64, 64)), ("filt_b2", (DEPTH, 64)), ("filt_w3", (DEPTH, 64, 2048)), ("filt_freq", (DEPTH, 64)),
    ("filt_decay", (DEPTH, 2048)), ("hyena_bias", (DEPTH, 2, 512)), ("attn_out_norm_w", (DEPTH, 512)),
    ("hyena_out_norm_w", (DEPTH, 512)), ("w_out", (DEPTH, 1024, 1024)), ("final_norm_w", (1024,)),
]


def build_program(seq_lens, depth=DEPTH, stop_after=None, debug=False):
    nc = bass.Bass("TRN2", target_bir_lowering=False)
    env = Env()
    env.nc = nc
    env.S = Sched(nc)
    OP, DMA = mk_ops(env)
    for name, shp in WEIGHT_SPECS:
        setattr(env, name, nc.dram_tensor(name, list(shp), F32, kind="ExternalInput").ap())
    env.winb = nc.dram_tensor("winb_scr", [DEPTH, 128, 8, DIN], BF16, kind="Internal").ap()
    env.woutb = nc.dram_tensor("woutb_scr", [DEPTH, 128, 8, D_MODEL], BF16, kind="Internal").ap()
    identb_d = nc.dram_tensor("identb", [128, 128], BF16, kind="ExternalInput").ap()
    identf_d = nc.dram_tensor("identf", [128, 128], F32, kind="ExternalInput").ap()
    consts = {}
    seqs = []
    for si, L in enumerate(seq_lens):
        sd = Env()
        sd.L = L
        sd.NT = L // 128
        sd.cfg = fft_cfg(L)
        NT, K1r, K1, CPB, CB, batches = sd.cfg
        if L not in consts:
            cs = make_consts(L)
            consts[L] = {}
            for k, v in cs.items():
                dt = BF16 if v.dtype == ml_dtypes.bfloat16 else F32
                consts[L][k] = nc.dram_tensor(f"c{L}_{k}", list(v.shape), dt, kind="ExternalInput").ap()
        sd.c = consts[L]
        sd.x_in = nc.dram_tensor(f"x{si}", [L, D_MODEL], F32, kind="ExternalInput").ap()
        sd.y_out = nc.dram_tensor(f"y{si}", [L, D_MODEL], F32, kind="ExternalOutput").ap()

        def scr(nm, shp, dt):
            return nc.dram_tensor(f"s{si}_{nm}", list(shp), dt, kind=("ExternalOutput" if debug else "Internal")).ap()
        sd.xres = scr("xres", [L, D_MODEL], F32)
        sd.qT = scr("qT", [5, 128, L], BF16)
        sd.vtok = scr("vtok", [L, 128], BF16)
        sd.ga = scr("ga", [L, 512], BF16)
        sd.uT = scr("uT", [1536, L], BF16)
        sd.ghT = scr("ghT", [512, L], BF16)
        sd.ucT = scr("ucT", [1024, L], BF16)
        sd.vbf = scr("vbf", [512, L], BF16)
        sd.zbf = scr("zbf", [512, L], BF16)
        sd.ybuf = scr("ybuf", [512, L], F32)
        sd.hbf = scr("hbf", [2, 2, 512, L], BF16)
        sd.Hd = scr("Hd", [2, len(batches), 2, 128, CB * K1], BF16)
        sd.oT = scr("oT", [8, 128, L], BF16)
        seqs.append(sd)
    env.stopped = False
    with ExitStack() as es:
        es.enter_context(nc.allow_non_contiguous_dma("param / layout loads"))

        def psb(name, shape, dt):
            return T(es.enter_context(nc.sbuf_tensor(name, shape, dt)), env.S.buf())
        env.identb = psb("identb_s", [128, 128], BF16)
        env.identf = psb("identf_s", [128, 128], F32)
        env.onesb = psb("onesb_s", [128, 128], BF16)
        env.rnorm = [psb("rnorm_s0", [128, 8], F32), psb("rnorm_s1", [128, 8], F32)]
        DMA("sp", env.identb[:], identb_d, w=[env.identb])
        DMA("sp", env.identf[:], identf_d, w=[env.identf])
        OP("pool", lambda e: e.memset(env.onesb[:], 1.0), w=[env.onesb])
        env.S.flush()
        phase_prep(env, depth)
        for sd in seqs:
            for l in range(depth):
                last = (l == depth - 1)
                steps = [
                    ("inproj", lambda: phase_inproj(env, sd, l)),
                    ("attn", lambda: phase_attn(env, sd, l)),
                    ("filters", (lambda: phase_filters(env, sd, l)) if l == 0 else (lambda: None)),
                    ("filter_fft", lambda: phase_filter_fft(env, sd, l)),
                    ("fftconv0", lambda: phase_fftconv(env, sd, l, 0)),
                    ("gate0", lambda: phase_gate(env, sd, l, 0)),
                    ("fftconv1", lambda: phase_fftconv(env, sd, l, 1)),
                    ("gate1", lambda: phase_gate(env, sd, l, 1)),
                    ("outproj", lambda: phase_outproj(env, sd, l, last)),
                ]
                for nm, fn in steps:
                    if env.stopped:
                        break
                    fn()
                    if stop_after == nm:
                        env.stopped = True
    env.consts_np = {L: make_consts(L) for L in consts}
    return nc, env


def make_in_maps(inputs, seq_lens, xs_per_core):
    base = {name: np.ascontiguousarray(np.asarray(inputs[name], dtype=np.float32)) for name, _ in WEIGHT_SPECS}
    base["identb"] = np.eye(128, dtype=np.float32).astype(ml_dtypes.bfloat16)
    base["identf"] = np.eye(128, dtype=np.float32)
    for L in set(seq_lens):
        for k, v in make_consts(L).items():
            base[f"c{L}_{k}"] = np.ascontiguousarray(v)
    maps = []
    for xs in xs_per_core:
        m = dict(base)
        for si, x in enumerate(xs):
            m[f"x{si}"] = np.ascontiguousarray(x, dtype=np.float32)
        maps.append(m)
    return maps


_PROG_CACHE = {}


def kernel(**inputs):
    xp = np.asarray(inputs["x_prompt"], dtype=np.float32)
    xs = np.asarray(inputs["x_sample"], dtype=np.float32)
    B = xp.shape[0]
    seq_lens = (xp.shape[1], xs.shape[1])
    if seq_lens not in _PROG_CACHE:
        _PROG_CACHE[seq_lens] = build_program(list(seq_lens))[0]
    nc = _PROG_CACHE[seq_lens]
    maps = make_in_maps(inputs, seq_lens, [[xp[b], xs[b]] for b in range(B)])
    res = run_bass_kernel_spmd(nc, maps, core_ids=list(range(B)))
    yp = np.stack([np.asarray(r["y0"], dtype=np.float32) for r in res.results], 0)
    ys = np.stack([np.asarray(r["y1"], dtype=np.float32) for r in res.results], 0)
    return (yp, ys)
```

```python
import numpy as np
import ml_dtypes
from contextlib import ExitStack
import concourse.bass as bass
import concourse.mybir as mybir
from concourse.bass_utils import run_bass_kernel_spmd

F32 = mybir.dt.float32
BF16 = mybir.dt.bfloat16
ALU = mybir.AluOpType
AF = mybir.ActivationFunctionType
AX = mybir.AxisListType

ENGS = ("pe", "act", "dve", "pool", "sp")
SEM_MAXV = 30000

D_MODEL = 1024
DIN = 3328
DEPTH = 4
EPS = 1e-6
PI = float(np.pi)


class Buf:
    __slots__ = ("w", "r", "rd")

    def __init__(self):
        self.w = None
        self.r = {}
        self.rd = []


class Op:
    __slots__ = ("eng", "fn", "deps", "sig", "dma", "sem", "val", "prev")

    def __init__(self, eng, fn, dma):
        self.eng = eng
        self.fn = fn
        self.dma = dma
        self.deps = ()
        self.sig = dma
        self.sem = None
        self.val = 0
        self.prev = 0


class Sched:
    def __init__(self, nc):
        self.nc = nc
        self.ops = {e: [] for e in ENGS}
        self.last = {e: None for e in ENGS}
        self.dmas = []
        self.bufs = []
        self.esems = {e: [] for e in ENGS}
        self.ecount = {e: 0 for e in ENGS}
        nd = {"sp": 24, "pool": 16, "act": 8}
        self.dsems = {e: [nc.alloc_semaphore(name=f"d_{e}_{i}") for i in range(n)] for e, n in nd.items()}
        self.dcnt = {e: [0] * n for e, n in nd.items()}
        self.drr = {e: 0 for e in nd}
        self.seen = {e: {} for e in ENGS}
        self.n_inst = 0

    def buf(self):
        b = Buf()
        self.bufs.append(b)
        return b

    def op(self, eng, fn, reads=(), writes=(), dma=False):
        o = Op(eng, fn, dma)
        raw = set()
        deps = set()
        for b in reads:
            if b.w is not None:
                raw.add(b.w)
        for b in writes:
            if b.w is not None:
                deps.add(b.w)
            deps.update(b.r.values())
            deps.update(b.rd)
        deps -= raw
        if dma:
            dl = list(raw) + list(deps)
        else:
            dl = list(raw) + [d for d in deps if d.dma or d.eng != eng or eng != "pe"]
        for d in dl:
            d.sig = True
        o.deps = dl
        for b in reads:
            if dma:
                b.rd.append(o)
            else:
                b.r[eng] = o
        for b in writes:
            b.w = o
            b.r = {}
            b.rd = []
        self.ops[eng].append(o)
        if dma:
            self.dmas.append(o)
        else:
            self.last[eng] = o
        return o

    def flush(self):
        nc = self.nc
        lasts = [o for o in self.last.values() if o is not None]
        for e in ENGS:
            o = Op(e, None, False)
            o.deps = [d for d in lasts if d.eng != e] + list(self.dmas)
            for d in o.deps:
                d.sig = True
            self.ops[e].append(o)
        for e in ENGS:
            for o in self.ops[e]:
                if o.fn is None or not o.sig:
                    continue
                if o.dma:
                    i = self.drr[e]
                    self.drr[e] = (i + 1) % len(self.dsems[e])
                    o.sem = self.dsems[e][i]
                    o.prev = self.dcnt[e][i]
                    self.dcnt[e][i] += 16
                    o.val = self.dcnt[e][i]
                else:
                    c = self.ecount[e]
                    ep, v = divmod(c, SEM_MAXV)
                    while len(self.esems[e]) <= ep:
                        self.esems[e].append(nc.alloc_semaphore(name=f"e_{e}_{len(self.esems[e])}"))
                    o.sem = self.esems[e][ep]
                    o.val = v + 1
                    self.ecount[e] = c + 1

        def emit(e, h):
            seen = self.seen[e]
            for o in self.ops[e]:
                waits = {}
                for d in o.deps:
                    k = id(d.sem)
                    if seen.get(k, 0) >= d.val:
                        continue
                    if k not in waits or waits[k][1] < d.val:
                        waits[k] = (d.sem, d.val)
                if o.dma and o.prev > 0 and seen.get(id(o.sem), 0) < o.prev:
                    k = id(o.sem)
                    if k not in waits or waits[k][1] < o.prev:
                        waits[k] = (o.sem, o.prev)
                for k, (s, v) in waits.items():
                    h.wait_ge(s, v)
                    seen[k] = v
                    self.n_inst += 1
                if o.fn is None:
                    continue
                ins = o.fn(h)
                self.n_inst += 1
                if o.sig:
                    ins.then_inc(o.sem, 16 if o.dma else 1)

        with nc.Block() as block:
            @block.tensor
            def _(h):
                emit("pe", h)

            @block.scalar
            def _(h):
                emit("act", h)

            @block.vector
            def _(h):
                emit("dve", h)

            @block.gpsimd
            def _(h):
                emit("pool", h)

            @block.sync
            def _(h):
                emit("sp", h)
        self.ops = {e: [] for e in ENGS}
        self.last = {e: None for e in ENGS}
        self.dmas = []
        for b in self.bufs:
            b.w = None
            b.r = {}
            b.rd = []


class T:
    __slots__ = ("t", "b")

    def __init__(self, t, b):
        self.t = t
        self.b = b

    def __getitem__(self, k):
        return self.t[k]


class Env:
    pass


_uid = [0]


def uid():
    _uid[0] += 1
    return _uid[0]


class Phase:
    def __init__(self, env, name):
        self.env = env
        self.es = ExitStack()
        self.name = name

    def sb(self, shape, dt):
        t = self.es.enter_context(self.env.nc.sbuf_tensor(f"{self.name}_s{uid()}", list(shape), dt))
        return T(t, self.env.S.buf())

    def ps(self, shape, dt):
        t = self.es.enter_context(self.env.nc.psum_tensor(f"{self.name}_p{uid()}", list(shape), dt))
        return T(t, self.env.S.buf())

    def alias(self, tile):
        return T(tile.t, self.env.S.buf())

    def end(self):
        self.env.S.flush()
        self.es.close()


def mk_ops(env):
    S = env.S

    def OP(eng, fn, r=(), w=()):
        return S.op(eng, fn, [x.b for x in r], [x.b for x in w])

    def DMA(eng, out, in_, r=(), w=()):
        return S.op(eng, lambda e: e.dma_start(out=out, in_=in_), [x.b for x in r], [x.b for x in w], dma=True)

    return OP, DMA


def fft_cfg(L):
    NT = L // 128
    K1r = NT + 1
    K1 = K1r + (K1r % 2)
    CPB = min(512 // (2 * K1), 8)
    CB = CPB * 4
    for m_ in range(max(1, 28 // CPB), 0, -1):
        if (CPB * m_) % 4 == 0:
            CB = CPB * m_
            break
    batches = []
    c = 0
    while c < 512:
        n = min(CB, 512 - c)
        batches.append((c, n))
        c += n
    return NT, K1r, K1, CPB, CB, batches


def make_consts(L):
    bf = ml_dtypes.bfloat16
    NT, K1r, K1, CPB, CB, batches = fft_cfg(L)
    N1 = 2 * NT
    N = 2 * L
    c = {}
    t = np.arange(L)
    pos = np.stack([t // 64, t % 64], -1).astype(np.float32)
    freqs = (10000.0 ** (-np.arange(16, dtype=np.float32) / 16)).astype(np.float32)
    ang = pos[:, :, None] * freqs[None, None, :]
    c["ropec"] = np.cos(ang).astype(np.float32).reshape(L, 32)
    c["ropes"] = np.sin(ang).astype(np.float32).reshape(L, 32)
    tt = np.linspace(0.0, 1.0, L, dtype=np.float32)
    w = (2.0 * np.pi * np.arange(L, dtype=np.float32) / L).astype(np.float32)
    bands = np.linspace(1e-4, 15.0, 16, dtype=np.float32)
    a2 = (w[:, None] * bands[None, :]).astype(np.float32)
    z = np.concatenate([tt[:, None], np.cos(a2), -np.sin(a2)], -1).astype(np.float32)
    c["zT"] = np.ascontiguousarray(z.T)
    c["tv"] = tt[None, :].copy()
    n1 = np.arange(NT)[:, None].astype(np.float64)
    k1 = np.arange(K1)[None, :].astype(np.float64)
    valid = (np.arange(K1) < K1r).astype(np.float64)[None, :]
    a = 2 * np.pi * n1 * k1 / N1
    c["F1"] = np.concatenate([np.cos(a) * valid, -np.sin(a) * valid], 1).astype(bf)
    n2 = np.arange(128)[:, None].astype(np.float64)
    a = 2 * np.pi * n2 * k1 / N
    c["twr"] = (np.cos(a) * valid).astype(bf)
    c["twi"] = (-np.sin(a) * valid).astype(bf)
    k2 = np.arange(128)[None, :].astype(np.float64)
    a = 2 * np.pi * n2 * k2 / 128
    C2, S2 = np.cos(a), np.sin(a)
    c["c2"] = np.stack([C2, S2, -S2, -C2], 1).astype(bf)
    c["e12"] = np.stack([np.concatenate([C2, S2], 1), np.concatenate([-S2, C2], 1), np.concatenate([-C2, -S2], 1)], 1).astype(bf)
    a = 2 * np.pi * k1.T * n2.T / N
    c["tir"] = (np.cos(a) * valid.T).astype(bf)
    c["tii"] = (np.sin(a) * valid.T).astype(bf)
    wk = np.full((K1, 1), 2.0)
    wk[0, 0] = 1.0
    wk[NT, 0] = 1.0
    wk = wk * valid.T
    a = 2 * np.pi * k1.T * n1.T / N1
    c1w = np.zeros((128, 3, NT))
    c1w[:K1] = np.stack([wk * np.cos(a) / N, -wk * np.sin(a) / N, -wk * np.cos(a) / N], 1)
    c["c1w"] = c1w.astype(bf)
    return c


CONST_SHAPES = None


def phase_prep(env, depth):
    OP, DMA = mk_ops(env)
    ph = Phase(env, "prep")
    wbf = ph.sb([128, 8, DIN], BF16)
    wb = [ph.alias(wbf) for _ in range(8)]
    wob = ph.sb([128, 8, D_MODEL], BF16)
    wbb = [ph.alias(wob) for _ in range(8)]
    nws = [ph.sb([128, 8], F32) for _ in range(2)]
    stg = [ph.sb([128, DIN], F32) for _ in range(3)]
    stq = [ph.alias(s_) for s_ in stg]
    sto = [ph.sb([128, D_MODEL], F32) for _ in range(2)]
    k = 0
    for l in range(depth):
        nw = nws[l % 2]
        DMA("sp", nw[:], env.norm_w[l, :].rearrange("(c p) -> p c", p=128), w=[nw])
        for dc in range(8):
            st, sq_ = stg[k % 3], stq[k % 3]
            k += 1
            src = env.w_in[l, dc * 128:(dc + 1) * 128, :]
            DMA("sp", st[:, 0:512].rearrange("p (j hh d) -> p j hh d", j=4, hh=2)[:, :, 0, :],
                src[:, 0:512].rearrange("p (hh j d) -> p hh j d", hh=2, j=4)[:, 0, :, :], w=[sq_])
            DMA("sp", st[:, 0:512].rearrange("p (j hh d) -> p j hh d", j=4, hh=2)[:, :, 1, :],
                src[:, 0:512].rearrange("p (hh j d) -> p hh j d", hh=2, j=4)[:, 1, :, :], w=[sq_])
            DMA("pool", st[:, 512:DIN], src[:, 512:DIN], w=[st])
            if dc % 8 in (0, 3, 5):
                OP("act", lambda e, dc=dc, st=st, nw=nw: e.activation(out=wbf[:, dc, :], in_=st[:], func=AF.Copy, scale=nw[:, dc:dc + 1]),
                   r=[st, sq_, nw], w=[wb[dc]])
            else:
                eng = "dve" if dc % 8 in (1, 4, 6, 7) else "pool"
                OP(eng, lambda e, dc=dc, st=st, nw=nw: e.tensor_scalar(out=wbf[:, dc, :], in0=st[:], scalar1=nw[:, dc:dc + 1],
                                                                       scalar2=None, op0=ALU.mult), r=[st, sq_, nw], w=[wb[dc]])
            DMA("sp", env.winb[l, :, dc, :], wbf[:, dc, :], r=[wb[dc]])
        for dc in range(8):
            st = sto[dc % 2]
            DMA("pool", st[:], env.w_out[l, dc * 128:(dc + 1) * 128, :], w=[st])
            eng = "dve" if dc % 2 == 0 else "act"
            if eng == "dve":
                OP("dve", lambda e, dc=dc, st=st: e.tensor_copy(out=wob[:, dc, :], in_=st[:]), r=[st], w=[wbb[dc]])
            else:
                OP("act", lambda e, dc=dc, st=st: e.activation(out=wob[:, dc, :], in_=st[:], func=AF.Copy), r=[st], w=[wbb[dc]])
            DMA("sp", env.woutb[l, :, dc, :], wob[:, dc, :], r=[wbb[dc]])
    ph.end()


def phase_inproj(env, sd, l):
    nc, S = env.nc, env.S
    OP, DMA = mk_ops(env)
    L, NT = sd.L, sd.NT
    x_src = sd.x_in if l == 0 else sd.xres
    ph = Phase(env, "p1")
    wbf = ph.sb([128, 8, DIN], BF16)
    wb = [ph.alias(wbf) for _ in range(8)]
    for dc in range(8):
        DMA("sp" if dc % 2 == 0 else "act", wbf[:, dc, :], env.winb[l, :, dc, :], w=[wb[dc]])
    cosT = ph.sb([128, NT, 32], F32)
    sinT = ph.sb([128, NT, 32], F32)
    DMA("sp", cosT[:], sd.c["ropec"].rearrange("(t p) k -> p t k", p=128), w=[cosT])
    DMA("sp", sinT[:], sd.c["ropes"].rearrange("(t p) k -> p t k", p=128), w=[sinT])
    tq = ph.sb([128, 64], F32)
    tk = ph.sb([128, 64], F32)
    DMA("sp", tq[:], env.q_norm_w[l:l + 1, :].partition_broadcast(128), w=[tq])
    DMA("sp", tk[:], env.k_norm_w[l:l + 1, :].partition_broadcast(128), w=[tk])
    wqk = ph.sb([128, 10, 64], F32)
    OP("dve", lambda e: e.tensor_scalar(out=wqk[:, 0:8, :], in0=tq[:].unsqueeze(1).to_broadcast([128, 8, 64]),
                                        scalar1=0.125, scalar2=None, op0=ALU.mult), r=[tq], w=[wqk])
    OP("dve", lambda e: e.tensor_copy(out=wqk[:, 8:10, :], in_=tk[:].unsqueeze(1).to_broadcast([128, 2, 64])), r=[tk], w=[wqk])
    identb = env.identb
    xsets = [[ph.sb([128, D_MODEL], F32) for _ in range(4)] for _ in range(2)]
    sqscr = ph.sb([128, D_MODEL], F32)
    ssx = [ph.sb([128, 4], F32) for _ in range(2)]
    rs = [ph.sb([128, 4], F32) for _ in range(2)]
    hbs = [ph.sb([128, D_MODEL], BF16) for _ in range(2)]
    hTs = [ph.sb([128, 8, 512], BF16) for _ in range(2)]
    hTq = [[ph.alias(hTs[s_]) for _ in range(4)] for s_ in range(2)]
    qk = [ph.sb([128, 10, 64], F32) for _ in range(2)]
    sq2 = ph.sb([128, 10, 64], F32)
    ssq = [ph.sb([128, 10], F32) for _ in range(2)]
    rt = [ph.sb([128, 10, 2, 16], F32) for _ in range(4)]
    qkr = [ph.sb([128, 10, 2, 2, 16], BF16) for _ in range(4)]
    qTst = [ph.sb([128, 5, 512], BF16) for _ in range(2)]
    vst = [ph.sb([128, 4, 128], BF16) for _ in range(2)]
    vstq = [[ph.alias(vst[s_]) for _ in range(4)] for s_ in range(2)]
    gast = [ph.sb([128, 4, 512], BF16) for _ in range(2)]
    gastq = [[ph.alias(gast[s_]) for _ in range(4)] for s_ in range(2)]
    ust = [ph.sb([128, 512], BF16) for _ in range(4)]
    ghst = [ph.sb([128, 512], BF16) for _ in range(2)]
    pT = [ph.ps([128, 8, 128], BF16) for _ in range(2)]
    pq = ph.ps([128, 512], F32)
    pkv = ph.ps([128, 512], F32)
    pga = ph.ps([128, 512], F32)
    pf = [ph.ps([128, 512], F32) for _ in range(2)]
    pqT = ph.ps([128, 8, 128], BF16)
    NG = L // 512
    ucnt = 0
    gcnt = 0
    for g in range(NG):
        s_ = g % 2
        xs = xsets[s_]
        for i in range(4):
            r0 = (g * 4 + i) * 128
            DMA("sp", xs[i][:], x_src[r0:r0 + 128, :], w=[xs[i]])
        OP("pool", lambda e, s_=s_: e.memset(ssx[s_][:], 0.0), w=[ssx[s_]])
        for i in range(4):
            OP("act", lambda e, i=i, xs=xs, s_=s_: e.activation(out=sqscr[:], in_=xs[i][:], func=AF.Square,
                                                               accum_out=ssx[s_][:, i:i + 1]), r=[xs[i], ssx[s_]], w=[sqscr, ssx[s_]])
        OP("dve", lambda e, s_=s_: e.tensor_scalar(out=rs[s_][:], in0=ssx[s_][:], scalar1=1.0 / D_MODEL, scalar2=EPS,
                                                  op0=ALU.mult, op1=ALU.add), r=[ssx[s_]], w=[rs[s_]])
        OP("act", lambda e, s_=s_: e.activation(out=rs[s_][:], in_=rs[s_][:], func=AF.Sqrt), r=[rs[s_]], w=[rs[s_]])
        OP("dve", lambda e, s_=s_: e.reciprocal(out=rs[s_][:], in_=rs[s_][:]), r=[rs[s_]], w=[rs[s_]])
        hT = hTs[s_]
        for i in range(4):
            hb = hbs[i % 2]
            OP("dve", lambda e, i=i, hb=hb, xs=xs, s_=s_: e.tensor_scalar(out=hb[:], in0=xs[i][:], scalar1=rs[s_][:, i:i + 1],
                                                                        scalar2=None, op0=ALU.mult), r=[xs[i], rs[s_]], w=[hb])
            p = pT[i % 2]
            for dc in range(8):
                OP("pe", lambda e, dc=dc, hb=hb, p=p: e.transpose(out=p[:, dc, :], in_=hb[:, dc * 128:(dc + 1) * 128],
                                                                identity=identb[:]), r=[hb, identb], w=[p])
            OP("act", lambda e, i=i, p=p, hT=hT: e.activation(out=hT[:, :, i * 128:(i + 1) * 128], in_=p[:], func=AF.Copy),
               r=[p], w=[hTq[s_][i]])
        for i in range(4):
            tix = g * 4 + i
            for (c0, cw, pb) in ((0, 512, pq), (512, 256, pkv), (768, 512, pga)):
                for dc in range(8):
                    OP("pe", lambda e, dc=dc, i=i, c0=c0, cw=cw, pb=pb, hT=hT: e.matmul(
                        pb[:, 0:cw], lhsT=hT[:, dc, i * 128:(i + 1) * 128], rhs=wbf[:, dc, c0:c0 + cw],
                        start=(dc == 0), stop=(dc == 7)), r=[hTq[s_][i], wb[dc]], w=[pb])
            q_ = qk[i % 2]
            OP("act", lambda e, q_=q_: e.activation(out=q_[:, 0:8, :], in_=pq[:].rearrange("p (h d) -> p h d", h=8), func=AF.Copy),
               r=[pq], w=[q_])
            OP("act", lambda e, q_=q_: e.activation(out=q_[:, 8:10, :], in_=pkv[:, 0:128].rearrange("p (h d) -> p h d", h=2),
                                                   func=AF.Copy), r=[pkv], w=[q_])
            OP("act", lambda e, i=i, s_=s_: e.activation(out=vst[s_][:, i, :], in_=pkv[:, 128:256], func=AF.Copy), r=[pkv], w=[vstq[s_][i]])
            OP("act", lambda e, i=i, s_=s_: e.activation(out=gast[s_][:, i, :], in_=pga[:], func=AF.Silu), r=[pga], w=[gastq[s_][i]])
            sq_ = ssq[i % 2]
            OP("dve", lambda e, q_=q_: e.tensor_tensor(out=sq2[:], in0=q_[:], in1=q_[:], op=ALU.mult), r=[q_], w=[sq2])
            OP("dve", lambda e, sq_=sq_: e.tensor_reduce(out=sq_[:], in_=sq2[:], axis=AX.X, op=ALU.add), r=[sq2], w=[sq_])
            OP("dve", lambda e, sq_=sq_: e.tensor_scalar(out=sq_[:], in0=sq_[:], scalar1=1.0 / 64, scalar2=EPS, op0=ALU.mult, op1=ALU.add),
               r=[sq_], w=[sq_])
            OP("act", lambda e, sq_=sq_: e.activation(out=sq_[:], in_=sq_[:], func=AF.Sqrt), r=[sq_], w=[sq_])
            OP("dve", lambda e, sq_=sq_: e.reciprocal(out=sq_[:], in_=sq_[:]), r=[sq_], w=[sq_])
            OP("dve", lambda e, q_=q_, sq_=sq_: e.tensor_tensor(out=q_[:], in0=q_[:], in1=sq_[:].unsqueeze(2).to_broadcast([128, 10, 64]),
                                                              op=ALU.mult), r=[q_, sq_], w=[q_])
            OP("pool", lambda e, q_=q_: e.tensor_tensor(out=q_[:], in0=q_[:], in1=wqk[:], op=ALU.mult), r=[q_, wqk], w=[q_])
            qv = q_[:].rearrange("p h (a m f) -> p h a m f", a=2, m=2)
            x1 = qv[:, :, :, 0, :]
            x2 = qv[:, :, :, 1, :]
            cb_ = cosT[:, tix, :].rearrange("p (a f) -> p a f", a=2).unsqueeze(1).to_broadcast([128, 10, 2, 16])
            sb_ = sinT[:, tix, :].rearrange("p (a f) -> p a f", a=2).unsqueeze(1).to_broadcast([128, 10, 2, 16])
            qo = qkr[(g * 4 + i) % 4]
            t1, t2, t3, t4 = rt
            OP("pool", lambda e, x1=x1, cb_=cb_: e.tensor_tensor(out=t1[:], in0=x1, in1=cb_, op=ALU.mult), r=[q_, cosT], w=[t1])
            OP("pool", lambda e, x2=x2, sb_=sb_: e.tensor_tensor(out=t2[:], in0=x2, in1=sb_, op=ALU.mult), r=[q_, sinT], w=[t2])
            OP("pool", lambda e, qo=qo: e.tensor_tensor(out=qo[:, :, :, 0, :], in0=t1[:], in1=t2[:], op=ALU.subtract), r=[t1, t2], w=[qo])
            OP("dve", lambda e, x2=x2, cb_=cb_: e.tensor_tensor(out=t3[:], in0=x2, in1=cb_, op=ALU.mult), r=[q_, cosT], w=[t3])
            OP("dve", lambda e, x1=x1, sb_=sb_: e.tensor_tensor(out=t4[:], in0=x1, in1=sb_, op=ALU.mult), r=[q_, sinT], w=[t4])
            OP("dve", lambda e, qo=qo: e.tensor_tensor(out=qo[:, :, :, 1, :], in0=t3[:], in1=t4[:], op=ALU.add), r=[t3, t4], w=[qo])
        for fc in range(16):
            p = pf[fc % 2]
            c0 = 1280 + fc * 128
            for dc in range(8):
                OP("pe", lambda e, dc=dc, c0=c0, p=p, hT=hT: e.matmul(p[:], lhsT=wbf[:, dc, c0:c0 + 128], rhs=hT[:, dc, :],
                                                                     start=(dc == 0), stop=(dc == 7)),
                   r=[wb[dc]] + hTq[s_], w=[p])
            if fc < 12:
                u = ust[ucnt % 4]
                ucnt += 1
                OP("dve", lambda e, u=u, p=p: e.tensor_copy(out=u[:], in_=p[:]), r=[p], w=[u])
                DMA("pool", sd.uT[fc * 128:(fc + 1) * 128, g * 512:(g + 1) * 512], u[:], r=[u])
            else:
                gh = ghst[gcnt % 2]
                gcnt += 1
                OP("act", lambda e, gh=gh, p=p: e.activation(out=gh[:], in_=p[:], func=AF.Silu), r=[p], w=[gh])
                DMA("pool", sd.ghT[(fc - 12) * 128:(fc - 11) * 128, g * 512:(g + 1) * 512], gh[:], r=[gh])
        for i in range(4):
            qo = qkr[(g * 4 + i) % 4]
            qf = qo[:].rearrange("p h a m f -> p (h a m f)")
            for c in range(5):
                OP("pe", lambda e, c=c, qf=qf: e.transpose(out=pqT[:, c, :], in_=qf[:, c * 128:(c + 1) * 128], identity=identb[:]),
                   r=[qo, identb], w=[pqT])
            OP("dve", lambda e, i=i, s_=s_: e.tensor_copy(out=qTst[s_][:, :, i * 128:(i + 1) * 128], in_=pqT[:, 0:5, :]), r=[pqT], w=[qTst[s_]])
        DMA("sp", sd.qT[:, :, g * 512:(g + 1) * 512].rearrange("c p t -> p c t"), qTst[s_][:], r=[qTst[s_]])
        DMA("sp", sd.vtok[g * 512:(g + 1) * 512, :].rearrange("(i p) f -> p i f", p=128), vst[s_][:], r=vstq[s_])
        DMA("sp", sd.ga[g * 512:(g + 1) * 512, :].rearrange("(i p) f -> p i f", p=128), gast[s_][:], r=gastq[s_])
    ph.end()


def phase_attn(env, sd, l):
    nc, S = env.nc, env.S
    OP, DMA = mk_ops(env)
    L, NT = sd.L, sd.NT
    ph = Phase(env, "p2")
    kT = ph.sb([128, L], BF16)
    DMA("sp", kT[:], sd.qT[4, :, :], w=[kT])
    vx = ph.sb([128, NT, 2, 128], BF16)
    OP("pool", lambda e: e.memset(vx[:], 1.0), w=[vx])
    for h_ in range(2):
        DMA("sp", vx[:, :, h_, 0:64], sd.vtok.rearrange("(t p) (h d) -> p t h d", p=128, h=2)[:, :, h_, :], w=[vx])
    awn = ph.sb([128, 512], F32)
    DMA("sp", awn[:], env.attn_out_norm_w[l:l + 1, :].partition_broadcast(128), w=[awn])
    identb, identf = env.identb, env.identf
    qts = [ph.sb([128, 4, 2, 512], BF16) for _ in range(2)]
    qtB = [ph.alias(q_) for q_ in qts]
    for q_ in qts:
        OP("pool", lambda e, q_=q_: e.memset(q_[64:128, :, 0, :], 0.0), w=[q_])
        OP("pool", lambda e, q_=q_: e.memset(q_[0:64, :, 1, :], 0.0), w=[q_])
    gats = [ph.sb([128, 4, 512], BF16) for _ in range(2)]
    otoks = [ph.sb([128, 4, 8, 65], F32) for _ in range(2)]
    pts = [ph.sb([128, 1024], BF16) for _ in range(3)]
    osbs = [ph.sb([128, 512], F32) for _ in range(2)]
    rden = ph.sb([128, 8], F32)
    of = ph.sb([128, 8, 64], F32)
    sqs = ph.sb([128, 512], F32)
    ssa = ph.sb([128, 4], F32)
    o2 = ph.sb([128, 512], F32)
    ob = [ph.sb([128, 512], BF16) for _ in range(2)]
    oast = [ph.sb([128, 4, 512], BF16) for _ in range(2)]
    psS = [ph.ps([128, 1024], F32) for _ in range(3)]
    psO = [ph.ps([128, 512], F32) for _ in range(1)]
    psE_full = ph.ps([128, 512], F32)
    psE = T(psE_full.t[:, 0:130].rearrange("p (q d) -> p q d", q=2), psE_full.b)
    psT = T(psE_full.t[:, 256:512].bitcast(BF16).rearrange("p (c t) -> p c t", c=4), ph.env.S.buf())
    NQ = L // 512
    NKG = NT // 2
    cnt = 0
    hcnt = 0
    cg = conv_gen(env, sd, l, ph)
    n_conv_yields = 2 * 12 * (L // min(L, 2048))
    conv_every = max(1, (NQ * 8 * NKG) // (n_conv_yields + 2))
    for qc in range(NQ):
        qt = qts[qc % 2]
        gat = gats[qc % 2]
        otok = otoks[qc % 2]
        qtb = qtB[qc % 2]
        DMA("sp", qt[0:64, :, 0, :], sd.qT[0:4, 0:64, qc * 512:(qc + 1) * 512].rearrange("c p t -> p c t"), w=[qt])
        DMA("sp", qt[64:128, :, 1, :], sd.qT[0:4, 64:128, qc * 512:(qc + 1) * 512].rearrange("c p t -> p c t"), w=[qtb])
        DMA("sp", gat[:], sd.ga[qc * 512:(qc + 1) * 512, :].rearrange("(i p) f -> p i f", p=128), w=[gat])
        steps = [(j, hh, kg) for j in range(4) for hh in range(2) for kg in range(NKG)]

        def QK(st, pS, qt=qt, qtb=qtb):
            j, hh, kg = st
            for u in range(2):
                kb = kg * 2 + u
                OP("pe", lambda e, u=u, kb=kb, hh=hh, j=j, pS=pS, qt=qt: e.matmul(
                    pS[:, u * 512:(u + 1) * 512], lhsT=kT[:, kb * 128:(kb + 1) * 128], rhs=qt[:, j, hh, :],
                    start=True, stop=True), r=[kT, qt, qtb], w=[pS])

        pending = []
        QK(steps[0], psS[cnt % 3])
        if len(steps) > 1:
            QK(steps[1], psS[(cnt + 1) % 3])
        for si, st in enumerate(steps):
            j, hh, kg = st
            pS = psS[cnt % 3]
            pt = pts[cnt % 3]
            if si + 2 < len(steps):
                QK(steps[si + 2], psS[(cnt + 2) % 3])
            OP("act", lambda e, pS=pS, pt=pt: e.activation(out=pt[:], in_=pS[:], func=AF.Exp), r=[pS], w=[pt])
            po = psO[0]
            for u in range(2):
                kb = kg * 2 + u
                OP("pe", lambda e, u=u, kb=kb, hh=hh, pt=pt, po=po: e.matmul(
                    po[:, :], lhsT=vx[:, kb, hh, :], rhs=pt[:, u * 512:(u + 1) * 512],
                    start=(kb == 0), stop=(kb == NT - 1)), r=[vx, pt], w=[po])
            cnt += 1
            if cnt % conv_every == 0:
                next(cg, None)
            if kg == 1 or NKG == 1:
                for fn in pending:
                    fn()
                pending = []
            if kg == NKG - 1:
                head = j + 4 * hh
                osb = osbs[hcnt % 2]
                OP("dve", lambda e, osb=osb, po=po: e.tensor_copy(out=osb[0:65, :], in_=po[0:65, :]), r=[po], w=[osb])

                def epi(osb=osb, head=head, otok=otok):
                    for rr in range(2):
                        for q2 in range(2):
                            qi = rr * 2 + q2
                            OP("pe", lambda e, qi=qi, q2=q2: e.transpose(out=psE[:, q2, :], in_=osb[0:65, qi * 128:(qi + 1) * 128],
                                                                        identity=identf[0:65, 0:65]), r=[osb, identf], w=[psE])
                        OP("dve", lambda e, rr=rr: e.tensor_copy(out=otok[:, rr * 2:rr * 2 + 2, head, :], in_=psE[:]), r=[psE], w=[otok])
                pending.append(epi)
                hcnt += 1
        for fn in pending:
            fn()
        OP("pool", lambda e: e.memset(ssa[:], 0.0), w=[ssa])
        oa = oast[qc % 2]
        for qi in range(4):
            OP("dve", lambda e, qi=qi, otok=otok: e.reciprocal(out=rden[:], in_=otok[:, qi, :, 64]), r=[otok], w=[rden])
            OP("dve", lambda e, qi=qi, otok=otok: e.tensor_tensor(out=of[:], in0=otok[:, qi, :, 0:64],
                                                                 in1=rden[:].unsqueeze(2).to_broadcast([128, 8, 64]), op=ALU.mult),
               r=[otok, rden], w=[of])
            off = of[:].rearrange("p h d -> p (h d)")
            OP("act", lambda e, qi=qi, off=off: e.activation(out=sqs[:], in_=off, func=AF.Square, accum_out=ssa[:, qi:qi + 1]),
               r=[of, ssa], w=[sqs, ssa])
            OP("dve", lambda e, qi=qi: e.tensor_scalar(out=ssa[:, qi:qi + 1], in0=ssa[:, qi:qi + 1], scalar1=1.0 / 512, scalar2=EPS,
                                                      op0=ALU.mult, op1=ALU.add), r=[ssa], w=[ssa])
            OP("act", lambda e, qi=qi: e.activation(out=ssa[:, qi:qi + 1], in_=ssa[:, qi:qi + 1], func=AF.Sqrt), r=[ssa], w=[ssa])
            OP("dve", lambda e, qi=qi: e.reciprocal(out=ssa[:, qi:qi + 1], in_=ssa[:, qi:qi + 1]), r=[ssa], w=[ssa])
            OP("dve", lambda e, qi=qi, off=off: e.scalar_tensor_tensor(out=o2[:], in0=off, scalar=ssa[:, qi:qi + 1], in1=awn[:],
                                                                      op0=ALU.mult, op1=ALU.mult), r=[of, ssa, awn], w=[o2])
            obt = ob[qi % 2]
            OP("dve", lambda e, qi=qi, obt=obt, gat=gat: e.tensor_tensor(out=obt[:], in0=o2[:], in1=gat[:, qi, :], op=ALU.mult),
               r=[o2, gat], w=[obt])
            for c in range(4):
                OP("pe", lambda e, c=c, obt=obt: e.transpose(out=psT[:, c, :], in_=obt[:, c * 128:(c + 1) * 128], identity=identb[:]),
                   r=[obt, identb], w=[psT])
            OP("act", lambda e, qi=qi, oa=oa: e.activation(out=oa[:, :, qi * 128:(qi + 1) * 128], in_=psT[:, :, :], func=AF.Copy), r=[psT], w=[oa])
        DMA("pool", sd.oT[0:4, :, qc * 512:(qc + 1) * 512].rearrange("c p t -> p c t"), oa[:], r=[oa])
    for _ in cg:
        pass
    ph.end()


def conv_gen(env, sd, l, ph):
    OP, DMA = mk_ops(env)
    L = sd.L
    cw = ph.sb([128, 12, 3], F32)
    cb = ph.sb([128, 12], F32)
    for k_ in range(3):
        DMA("sp", cw[:, :, k_], env.conv_w[l, k_, :].rearrange("(c p) -> p c", p=128), w=[cw])
    DMA("sp", cb[:], env.conv_b[l, :].rearrange("(c p) -> p c", p=128), w=[cb])
    TP = min(L, 2048)
    NP = L // TP
    us = [ph.sb([128, TP + 2], BF16) for _ in range(3)]
    os_ = [ph.sb([128, TP], F32) for _ in range(3)]
    obs = [ph.sb([128, TP], BF16) for _ in range(3)]
    cnt = 0
    for c in range(12):
        for pc in range(NP):
            u = us[cnt % 3]
            o = os_[cnt % 3]
            ob = obs[cnt % 3]
            cnt += 1
            t0 = pc * TP
            lo = max(t0 - 1, 0)
            hi = min(t0 + TP + 1, L)
            if pc == 0:
                OP("dve", lambda e, u=u: e.memset(u[:, 0:1], 0.0), w=[u])
            if pc == NP - 1:
                OP("dve", lambda e, u=u: e.memset(u[:, TP + 1:TP + 2], 0.0), w=[u])
            DMA("sp", u[:, lo - (t0 - 1):hi - (t0 - 1)], sd.uT[c * 128:(c + 1) * 128, lo:hi], w=[u])
            yield
            OP("dve", lambda e, u=u, o=o, c=c: e.tensor_scalar(out=o[:], in0=u[:, 1:TP + 1], scalar1=cw[:, c, 1:2], scalar2=cb[:, c:c + 1],
                                                              op0=ALU.mult, op1=ALU.add), r=[u, cw, cb], w=[o])
            OP("dve", lambda e, u=u, o=o, c=c: e.scalar_tensor_tensor(out=o[:], in0=u[:, 0:TP], scalar=cw[:, c, 0:1], in1=o[:],
                                                                     op0=ALU.mult, op1=ALU.add), r=[u, cw, o], w=[o])
            OP("dve", lambda e, u=u, o=o, ob=ob, c=c: e.scalar_tensor_tensor(out=ob[:], in0=u[:, 2:TP + 2], scalar=cw[:, c, 2:3], in1=o[:],
                                                                            op0=ALU.mult, op1=ALU.add), r=[u, cw, o], w=[ob])
            if c < 4:
                DMA("pool", sd.vbf[c * 128:(c + 1) * 128, t0:t0 + TP], ob[:], r=[ob])
            else:
                DMA("pool", sd.ucT[(c - 4) * 128:(c - 3) * 128, t0:t0 + TP], ob[:], r=[ob])
            yield


def phase_filters(env, sd, l):
    ph = Phase(env, "hf")
    for _ in filters_gen(env, sd, l, ph):
        pass
    ph.end()


def filters_gen(env, sd, l, ph):
    OP, DMA = mk_ops(env)
    L = sd.L
    w1 = ph.sb([33, 64], F32)
    w2 = ph.sb([64, 64], F32)
    w3 = ph.sb([64, 2048], F32)
    w3b = ph.sb([64, 2048], BF16)
    h2b = ph.sb([64, min(L, 2048)], BF16)
    DMA("sp", w1[:], env.filt_w1[l, :, :], w=[w1])
    DMA("sp", w2[:], env.filt_w2[l, :, :], w=[w2])
    DMA("sp", w3[:], env.filt_w3[l, :, :], w=[w3])
    OP("pool", lambda e: e.tensor_copy(out=w3b[:], in_=w3[:]), r=[w3], w=[w3b])
    fr = ph.sb([64, 1], F32)
    b1 = ph.sb([64, 1], F32)
    b2 = ph.sb([64, 1], F32)
    DMA("sp", fr[:], env.filt_freq[l, :].rearrange("(p o) -> p o", o=1), w=[fr])
    DMA("sp", b1[:], env.filt_b1[l, :].rearrange("(p o) -> p o", o=1), w=[b1])
    DMA("sp", b2[:], env.filt_b2[l, :].rearrange("(p o) -> p o", o=1), w=[b2])
    OP("dve", lambda e: e.tensor_tensor(out=b1[:], in0=b1[:], in1=fr[:], op=ALU.mult), r=[b1, fr], w=[b1])
    OP("dve", lambda e: e.tensor_tensor(out=b2[:], in0=b2[:], in1=fr[:], op=ALU.mult), r=[b2, fr], w=[b2])
    nad = ph.sb([128, 16], F32)
    DMA("sp", nad[:], env.filt_decay[l, :].rearrange("(q p) -> p q", p=128), w=[nad])
    nad2 = ph.sb([128, 16], F32)
    OP("dve", lambda e: e.tensor_scalar(out=nad2[:], in0=nad[:], scalar1=-1.0, scalar2=None, op0=ALU.mult), r=[nad], w=[nad2])
    OP("dve", lambda e: e.tensor_tensor(out=nad[:], in0=nad[:], in1=nad2[:], op=ALU.min), r=[nad, nad2], w=[nad])
    TP = min(L, 2048)
    NP = L // TP
    NS = TP // 512
    tv = ph.sb([128, TP], F32)
    zt = ph.sb([33, TP], F32)
    h1 = ph.sb([64, TP], F32)
    h2 = ph.sb([64, TP], F32)
    ta = ph.sb([64, 512], F32)
    tw = ph.sb([64, 512], F32)
    asum = ph.sb([128, 16, NP * NS], F32)
    OP("pool", lambda e: e.memset(asum[:], 0.0), w=[asum])
    wins = [ph.sb([128, 512], F32) for _ in range(2)]
    taps = [ph.sb([128, 512], F32) for _ in range(2)]
    junk = ph.sb([128, 512], F32)
    tbf = [ph.sb([128, 512], BF16) for _ in range(4)]
    pm = [ph.ps([128, 512], F32) for _ in range(2)]
    pt_ = [ph.ps([128, 512], F32) for _ in range(2)]

    def sin_layer(pp, frb, dst_t, sl):
        OP("dve", lambda e: e.tensor_scalar(out=ta[:], in0=pp[0:64, :], scalar1=fr[:, 0:1], scalar2=frb[:, 0:1], op0=ALU.mult, op1=ALU.add),
           r=[pp, fr, frb], w=[ta])
        for _ in range(2):
            OP("dve", lambda e: e.tensor_scalar(out=tw[:], in0=ta[:], scalar1=PI, scalar2=-2 * PI, op0=ALU.is_gt, op1=ALU.mult), r=[ta], w=[tw])
            OP("dve", lambda e: e.tensor_tensor(out=ta[:], in0=ta[:], in1=tw[:], op=ALU.add), r=[ta, tw], w=[ta])
            OP("dve", lambda e: e.tensor_scalar(out=tw[:], in0=ta[:], scalar1=-PI, scalar2=2 * PI, op0=ALU.is_lt, op1=ALU.mult), r=[ta], w=[tw])
            OP("dve", lambda e: e.tensor_tensor(out=ta[:], in0=ta[:], in1=tw[:], op=ALU.add), r=[ta, tw], w=[ta])
        OP("act", lambda e: e.activation(out=dst_t[:, sl], in_=ta[:], func=AF.Sin), r=[ta], w=[dst_t])

    cnt = 0
    deferred = []
    for pc in range(NP):
        t0 = pc * TP
        DMA("sp", zt[:], sd.c["zT"][:, t0:t0 + TP], w=[zt])
        DMA("sp", tv[:], sd.c["tv"][:, t0:t0 + TP].partition_broadcast(128), w=[tv])
        for sblk in range(NS):
            sl = slice(sblk * 512, (sblk + 1) * 512)
            p = pm[0]
            OP("pe", lambda e, p=p, sl=sl: e.matmul(p[0:64, :], lhsT=w1[:, :], rhs=zt[:, sl], start=True, stop=True), r=[w1, zt], w=[p])
            sin_layer(p, b1, h1, sl)
            p = pm[1]
            OP("pe", lambda e, p=p, sl=sl: e.matmul(p[0:64, :], lhsT=w2[:, :], rhs=h1[:, sl], start=True, stop=True), r=[w2, h1], w=[p])
            sin_layer(p, b2, h2, sl)
            OP("pool", lambda e, sl=sl: e.tensor_copy(out=h2b[:, sl], in_=h2[:, sl]), r=[h2], w=[h2b])
            yield
        for q in range(16):
            o_, dr, ch = q // 8, (q // 4) % 2, q % 4
            for sblk in range(NS):
                sl = slice(sblk * 512, (sblk + 1) * 512)
                p = pt_[cnt % 2]
                win = wins[cnt % 2]
                tap = taps[cnt % 2]
                tb = tbf[cnt % 4]
                cnt += 1
                OP("pe", lambda e, p=p, q=q, sl=sl: e.matmul(p[:], lhsT=w3b[:, q * 128:(q + 1) * 128], rhs=h2b[:, sl], start=True, stop=True),
                   r=[w3b, h2b], w=[p])
                OP("act", lambda e, win=win, q=q, sl=sl: e.activation(out=win[:], in_=tv[:, sl], func=AF.Exp, scale=nad[:, q:q + 1]),
                   r=[tv, nad], w=[win])
                OP("dve", lambda e, win=win, tb=tb, p=p: e.scalar_tensor_tensor(out=tb[:], in0=win[:], scalar=0.05, in1=p[:],
                                                                               op0=ALU.add, op1=ALU.mult), r=[win, p], w=[tb])
                if dr == 1 and pc == 0 and sblk == 0:
                    OP("dve", lambda e, tb=tb: e.memset(tb[:, 0:1], 0.0), w=[tb])
                idx = pc * NS + sblk
                for fn in deferred:
                    fn()
                deferred.clear()

                def fin(tb=tb, q=q, idx=idx, o_=o_, dr=dr, ch=ch, c0_=t0 + sblk * 512):
                    OP("act", lambda e: e.activation(out=junk[:], in_=tb[:], func=AF.Abs, accum_out=asum[:, q, idx:idx + 1]),
                       r=[tb, asum], w=[junk, asum])
                    DMA("sp", sd.hbf[o_, dr, ch * 128:(ch + 1) * 128, c0_:c0_ + 512], tb[:], r=[tb])
                deferred.append(fin)
                yield
    for fn in deferred:
        fn()
    deferred.clear()
    tot = ph.sb([128, 16], F32)
    OP("dve", lambda e: e.tensor_reduce(out=tot[:], in_=asum[:], axis=AX.X, op=ALU.add), r=[asum], w=[tot])
    tv4 = tot[:].rearrange("p (o d c) -> p o d c", o=2, d=2)
    rn = env.rnorm[l % 2]
    OP("dve", lambda e: e.tensor_tensor(out=rn[:].rearrange("p (o c) -> p o c", o=2), in0=tv4[:, :, 0, :], in1=tv4[:, :, 1, :], op=ALU.add),
       r=[tot], w=[rn])
    OP("dve", lambda e: e.reciprocal(out=rn[:], in_=rn[:]), r=[rn], w=[rn])
    yield


class FFTC:
    pass


def load_fft_consts(env, ph, sd):
    OP, DMA = mk_ops(env)
    NT, K1r, K1, CPB, CB, batches = sd.cfg
    f = FFTC()
    f.F1 = ph.sb([NT, 2 * K1], BF16)
    f.twr = ph.sb([128, K1], BF16)
    f.twi = ph.sb([128, K1], BF16)
    f.c2 = ph.sb([128, 4, 128], BF16)
    f.e12 = ph.sb([128, 3, 256], BF16)
    f.tir = ph.sb([K1, 128], BF16)
    f.tii = ph.sb([K1, 128], BF16)
    f.c1w = ph.sb([128, 3, NT], BF16)
    for nm in ("F1", "twr", "twi", "c2", "e12", "tir", "tii", "c1w"):
        t = getattr(f, nm)
        src = sd.c[nm]
        DMA("sp", t[:], src, w=[t])
    return f


def cmul4(env, src_re, src_im, tab_a, tab_b, outs, r, w):
    OP, DMA = mk_ops(env)
    for i_, (x, tb, o) in enumerate(((src_re, tab_a, outs[0]), (src_im, tab_b, outs[1]), (src_re, tab_b, outs[2]), (src_im, tab_a, outs[3]))):
        eng = "pool" if i_ == 3 else "dve"
        OP(eng, lambda e, x=x, tb=tb, o=o: e.tensor_tensor(out=o, in0=x, in1=tb, op=ALU.mult), r=r, w=[w[i_]])


def fft_stage1_tw(env, sd, f, X, ncols, TT, psA, cntA, EV):
    OP, DMA = mk_ops(env)
    NT, K1r, K1, CPB, CB, batches = sd.cfg
    W = CPB * 2 * K1
    c = 0
    while c < ncols:
        ng = min(2 * CPB, ncols - c)
        pa = psA[cntA % len(psA)]
        ev = EV[cntA % len(EV)]
        cntA += 1
        for s_ in range(ng):
            b_, o_ = divmod(s_, CPB)
            OP("pe", lambda e, s_=s_, c=c, pa=pa, b_=b_, o_=o_: e.matmul(pa[:, b_, o_ * 2 * K1:(o_ + 1) * 2 * K1], lhsT=X[0:NT, c + s_, :],
                                                                         rhs=f.F1[:, :], start=True, stop=True), r=[X, f.F1], w=[pa])
        if ng == 2 * CPB:
            OP("act", lambda e, pa=pa, ev=ev: e.activation(out=ev[:, :, 0:W], in_=pa[:, :, 0:W], func=AF.Copy), r=[pa], w=[ev])
            pv = ev[:, :, 0:W].rearrange("p b (c r k) -> p b c r k", c=CPB, r=2)
            are, aim = pv[:, :, :, 0, :], pv[:, :, :, 1, :]
            twr = f.twr[:].unsqueeze(1).unsqueeze(1).to_broadcast([128, 2, CPB, K1])
            twi = f.twi[:].unsqueeze(1).unsqueeze(1).to_broadcast([128, 2, CPB, K1])
            outs = [t[:, c * K1:(c + ng) * K1].rearrange("p (b c k) -> p b c k", b=2, c=CPB) for t in TT]
            cmul4(env, are, aim, twr, twi, outs, r=[ev, f.twr, f.twi], w=TT)
        else:
            done = 0
            while done < ng:
                nb = min(CPB, ng - done)
                b_ = done // CPB
                OP("act", lambda e, pa=pa, ev=ev, b_=b_, nb=nb: e.activation(out=ev[:, b_, 0:nb * 2 * K1], in_=pa[:, b_, 0:nb * 2 * K1], func=AF.Copy),
                   r=[pa], w=[ev])
                pv = ev[:, b_, 0:nb * 2 * K1].rearrange("p (c r k) -> p c r k", c=nb, r=2)
                are, aim = pv[:, :, 0, :], pv[:, :, 1, :]
                twr = f.twr[:].unsqueeze(1).to_broadcast([128, nb, K1])
                twi = f.twi[:].unsqueeze(1).to_broadcast([128, nb, K1])
                outs = [t[:, (c + done) * K1:(c + done + nb) * K1].rearrange("p (c k) -> p c k", c=nb) for t in TT]
                cmul4(env, are, aim, twr, twi, outs, r=[ev, f.twr, f.twi], w=TT)
                done += nb
        c += ng
    return cntA


SEQ_RE = ((0, 0), (3, 1), (1, 2), (1, 3))
SEQ_IM = ((0, 2), (0, 3), (2, 0), (1, 1))
SEQ_IM_NEG = ((3, 2), (3, 3), (1, 0), (2, 1))


def phase_filter_fft(env, sd, l):
    OP, DMA = mk_ops(env)
    L = sd.L
    NT, K1r, K1, CPB, CB, batches = sd.cfg
    ph = Phase(env, "hff")
    f = load_fft_consts(env, ph, sd)
    Xs = [[ph.sb([NT, CB, 128], BF16) for _ in range(2)] for _ in range(2)]
    TTs = [[[ph.sb([128, CB * K1], BF16) for _ in range(4)] for _ in range(2)] for _ in range(2)]
    Hs = [ph.sb([128, 2, CB * K1], BF16) for _ in range(2)]
    psA = [ph.ps([128, 2, 512], F32) for _ in range(2)]
    psB = [ph.ps([128, 2, 512], F32) for _ in range(2)]
    EV = [ph.sb([128, 2, 512], BF16) for _ in range(3)]
    st = {"cntA": 0, "cntB": 0}
    work = [(o_, bidx, c0, ncols) for o_ in range(2) for bidx, (c0, ncols) in enumerate(batches)]

    def load_x(i):
        o_, bidx, c0, ncols = work[i]
        Xp = Xs[i % 2]
        for dr in range(2):
            DMA("sp", Xp[dr][:, 0:ncols, :], sd.hbf[o_, dr, c0:c0 + ncols, :].rearrange("c (a b) -> a c b", b=128), w=[Xp[dr]])

    def stage_a(i):
        o_, bidx, c0, ncols = work[i]
        Xp = Xs[i % 2]
        for dr in range(2):
            st["cntA"] = fft_stage1_tw(env, sd, f, Xp[dr], ncols, TTs[i % 2][dr], psA, st["cntA"], EV)

    def stage_b(i):
        o_, bidx, c0, ncols = work[i]
        Hsb = Hs[i % 2]
        TTp = TTs[i % 2]
        tot = ncols * K1
        g0 = 0
        while g0 < tot:
            gw = min(512, tot - g0)
            pb = psB[st["cntB"] % 2]
            st["cntB"] += 1
            sl = slice(g0, g0 + gw)
            seq_re = [(mi, TTp[0][ti]) for mi, ti in SEQ_RE] + [(mi, TTp[1][ti]) for mi, ti in SEQ_RE]
            seq_im = [(mi, TTp[0][ti]) for mi, ti in SEQ_IM] + [(mi, TTp[1][ti]) for mi, ti in SEQ_IM_NEG]
            for ri, seq in enumerate((seq_re, seq_im)):
                for k, (mi, Bt) in enumerate(seq):
                    OP("pe", lambda e, ri=ri, mi=mi, Bt=Bt, k=k, pb=pb, sl=sl, gw=gw: e.matmul(
                        pb[:, ri, 0:gw], lhsT=f.c2[:, mi, :], rhs=Bt[:, sl], start=(k == 0), stop=(k == 7)), r=[f.c2, Bt], w=[pb])
            OP("dve", lambda e, pb=pb, sl=sl, gw=gw, Hsb=Hsb: e.tensor_copy(out=Hsb[:, :, sl], in_=pb[:, :, 0:gw]), r=[pb], w=[Hsb])
            g0 += gw
        DMA("sp", sd.Hd[o_, bidx, :, :, 0:tot].rearrange("r p x -> p r x"), Hsb[:, :, 0:tot], r=[Hsb])

    n = len(work)
    load_x(0)
    for t in range(n + 1):
        if t < n:
            stage_a(t)
        if t + 1 < n:
            load_x(t + 1)
        if t >= 1:
            stage_b(t - 1)
    ph.end()


def phase_fftconv(env, sd, l, o_):
    OP, DMA = mk_ops(env)
    L = sd.L
    NT, K1r, K1, CPB, CB, batches = sd.cfg
    src = sd.vbf if o_ == 0 else sd.zbf
    ph = Phase(env, "hc")
    f = load_fft_consts(env, ph, sd)
    Xs = [ph.sb([NT, CB, 128], BF16) for _ in range(2)]
    Hs = [ph.sb([128, 2, CB * K1], BF16) for _ in range(2)]
    TTs = [[ph.sb([128, CB * K1], BF16) for _ in range(4)] for _ in range(2)]
    UUs = [[ph.sb([128, CB * K1 + 128], BF16) for _ in range(4)] for _ in range(2)]
    VVs = [[ph.sb([128, CB * 128], BF16) for _ in range(4)] for _ in range(2)]
    for uu_ in UUs:
        for u_ in uu_:
            OP("pool", lambda e, u_=u_: e.memset(u_[:, CB * K1:CB * K1 + 128], 0.0), w=[u_])
    for vv_ in VVs:
        for v_ in vv_:
            OP("pool", lambda e, v_=v_: e.memset(v_[:], 0.0), w=[v_])
    ysb = [ph.sb([NT, CB * 128], F32) for _ in range(2)]
    psA = [ph.ps([128, 2, 512], F32) for _ in range(2)]
    psB = [ph.ps([128, 2, 512], F32) for _ in range(2)]
    EV = [ph.sb([128, 2, 512], BF16) for _ in range(4)]
    st = {"cntA": 0, "cntB": 0, "cntE": 0}

    def load_x(bi):
        c0, ncols = batches[bi]
        X = Xs[bi % 2]
        DMA("sp", X[:, 0:ncols, :], src[c0:c0 + ncols, :].rearrange("c (a b) -> a c b", b=128), w=[X])

    def load_h(bi):
        c0, ncols = batches[bi]
        H = Hs[bi % 2]
        DMA("sp", H[:, :, 0:ncols * K1], sd.Hd[o_, bi, :, :, 0:ncols * K1].rearrange("r p x -> p r x"), w=[H])

    def stage_a(bi):
        c0, ncols = batches[bi]
        X = Xs[bi % 2]
        st["cntA"] = fft_stage1_tw(env, sd, f, X, ncols, TTs[bi % 2], psA, st["cntA"], EV)

    def stage_b(bi):
        c0, ncols = batches[bi]
        H = Hs[bi % 2]
        TT = TTs[bi % 2]
        UU = UUs[bi % 2]
        tot = ncols * K1
        g0 = 0
        while g0 < tot:
            gw = min(512, tot - g0)
            pb = psB[st["cntB"] % 2]
            st["cntB"] += 1
            sl = slice(g0, g0 + gw)
            for ri, seq in enumerate((SEQ_RE, SEQ_IM)):
                for k, (mi, ti) in enumerate(seq):
                    Bt = TT[ti]
                    OP("pe", lambda e, ri=ri, mi=mi, Bt=Bt, k=k, pb=pb, sl=sl, gw=gw: e.matmul(
                        pb[:, ri, 0:gw], lhsT=f.c2[:, mi, :], rhs=Bt[:, sl], start=(k == 0), stop=(k == 3)), r=[f.c2, Bt], w=[pb])
            ev = EV[st["cntE"] % 4]
            st["cntE"] += 1
            OP("act", lambda e, pb=pb, ev=ev, gw=gw: e.activation(out=ev[:, :, 0:gw], in_=pb[:, :, 0:gw], func=AF.Copy), r=[pb], w=[ev])
            cmul4(env, ev[:, 0, 0:gw], ev[:, 1, 0:gw], H[:, 0, sl], H[:, 1, sl], [u[:, sl] for u in UU], r=[ev, H], w=UU)
            g0 += gw

    def stage_c(bi):
        c0, ncols = batches[bi]
        UU = UUs[bi % 2]
        VV = VVs[bi % 2]
        c = 0
        while c < ncols:
            ng = min(4, ncols - c)
            pa = psA[st["cntA"] % 2]
            st["cntA"] += 1
            for s_ in range(ng):
                col = c + s_
                b_, o2 = divmod(s_, 2)
                for k, (ui, ei) in enumerate(((0, 0), (1, 2), (2, 1), (3, 1))):
                    U = UU[ui]
                    OP("pe", lambda e, b_=b_, o2=o2, col=col, pa=pa, U=U, ei=ei, k=k: e.matmul(
                        pa[:, b_, o2 * 256:(o2 + 1) * 256], lhsT=U[:, col * K1:col * K1 + 128], rhs=f.e12[:, ei, :],
                        start=(k == 0), stop=(k == 3)), r=[U, f.e12], w=[pa])
            ev = EV[st["cntE"] % 4]
            st["cntE"] += 1
            if ng == 4:
                OP("act", lambda e, pa=pa, ev=ev: e.activation(out=ev[0:K1, :, :], in_=pa[0:K1, :, :], func=AF.Copy), r=[pa], w=[ev])
                pv = ev[0:K1, :, :].rearrange("p b (c r n) -> p b c r n", c=2, r=2)
                zre, zim = pv[:, :, :, 0, :], pv[:, :, :, 1, :]
                tir = f.tir[:].unsqueeze(1).unsqueeze(1).to_broadcast([K1, 2, 2, 128])
                tii = f.tii[:].unsqueeze(1).unsqueeze(1).to_broadcast([K1, 2, 2, 128])
                outs = [v[0:K1, c * 128:(c + 4) * 128].rearrange("p (b c n) -> p b c n", b=2, c=2) for v in VV]
                cmul4(env, zre, zim, tir, tii, outs, r=[ev, f.tir, f.tii], w=VV)
            else:
                done = 0
                while done < ng:
                    nb = min(2, ng - done)
                    b_ = done // 2
                    OP("act", lambda e, pa=pa, ev=ev, b_=b_, nb=nb: e.activation(out=ev[0:K1, b_, 0:nb * 256], in_=pa[0:K1, b_, 0:nb * 256], func=AF.Copy),
                       r=[pa], w=[ev])
                    pv = ev[0:K1, b_, 0:nb * 256].rearrange("p (c r n) -> p c r n", c=nb, r=2)
                    zre, zim = pv[:, :, 0, :], pv[:, :, 1, :]
                    tir = f.tir[:].unsqueeze(1).to_broadcast([K1, nb, 128])
                    tii = f.tii[:].unsqueeze(1).to_broadcast([K1, nb, 128])
                    outs = [v[0:K1, (c + done) * 128:(c + done + nb) * 128].rearrange("p (c n) -> p c n", c=nb) for v in VV]
                    cmul4(env, zre, zim, tir, tii, outs, r=[ev, f.tir, f.tii], w=VV)
                    done += nb
            c += ng

    def stage_d(bi):
        c0, ncols = batches[bi]
        VV = VVs[bi % 2]
        ys = ysb[bi % 2]
        tot2 = ncols * 128
        g0 = 0
        while g0 < tot2:
            gw = min(512, tot2 - g0)
            pb = psB[st["cntB"] % 2]
            st["cntB"] += 1
            sl = slice(g0, g0 + gw)
            for k, (vi, wi) in enumerate(((0, 0), (1, 2), (2, 1), (3, 1))):
                V = VV[vi]
                OP("pe", lambda e, pb=pb, sl=sl, gw=gw, V=V, wi=wi, k=k: e.matmul(pb[0:NT, 0, 0:gw], lhsT=f.c1w[:, wi, :], rhs=V[:, sl],
                                                                              start=(k == 0), stop=(k == 3)), r=[f.c1w, V], w=[pb])
            OP("dve", lambda e, pb=pb, sl=sl, gw=gw, ys=ys: e.tensor_copy(out=ys[:, sl], in_=pb[0:NT, 0, 0:gw]), r=[pb], w=[ys])
            g0 += gw
        DMA("sp", sd.ybuf[c0:c0 + ncols, :].rearrange("c (a b) -> a c b", b=128), ys[:, 0:tot2].rearrange("p (c b) -> p c b", b=128), r=[ys])

    n = len(batches)
    load_x(0)
    load_h(0)
    for t in range(n + 3):
        if t < n:
            stage_a(t)
        if t + 1 < n:
            load_x(t + 1)
        if 0 <= t - 1 < n:
            stage_b(t - 1)
        if t + 1 < n:
            load_h(t + 1)
        if 0 <= t - 2 < n:
            stage_c(t - 2)
        if 0 <= t - 3 < n:
            stage_d(t - 3)
    ph.end()


def phase_gate(env, sd, l, o_):
    OP, DMA = mk_ops(env)
    L = sd.L
    ph = Phase(env, "hg")
    hbz = ph.sb([128, 2, 4], F32)
    for o2_ in range(2):
        DMA("sp", hbz[:, o2_, :], env.hyena_bias[l, o2_, :].rearrange("(c p) -> p c", p=128), w=[hbz])
    rn = env.rnorm[l % 2]
    TP = min(L, 1024)
    NP = L // TP
    vsrc = sd.vbf if o_ == 0 else sd.zbf
    if o_ == 0:
        ys = [ph.sb([128, TP], F32) for _ in range(2)]
        vs = [ph.sb([128, TP], BF16) for _ in range(2)]
        xs = [ph.sb([128, TP], BF16) for _ in range(2)]
        zs = [ph.sb([128, TP], BF16) for _ in range(2)]
        cnt = 0
        for c in range(4):
            for pc in range(NP):
                y, v, x, z = ys[cnt % 2], vs[cnt % 2], xs[cnt % 2], zs[cnt % 2]
                eng = "dve"
                cnt += 1
                t0 = pc * TP
                rows = slice(c * 128, (c + 1) * 128)
                DMA("sp", y[:], sd.ybuf[rows, t0:t0 + TP], w=[y])
                DMA("sp", v[:], vsrc[rows, t0:t0 + TP], w=[v])
                DMA("sp", x[:], sd.ucT[rows, t0:t0 + TP], w=[x])
                OP(eng, lambda e, y=y, c=c: e.tensor_scalar(out=y[:], in0=y[:], scalar1=rn[:, c:c + 1], scalar2=None, op0=ALU.mult), r=[y, rn], w=[y])
                OP(eng, lambda e, y=y, v=v, c=c: e.scalar_tensor_tensor(out=y[:], in0=v[:], scalar=hbz[:, 0, c:c + 1], in1=y[:], op0=ALU.mult, op1=ALU.add),
                   r=[y, v, hbz], w=[y])
                OP(eng, lambda e, y=y, x=x, z=z: e.tensor_tensor(out=z[:], in0=x[:], in1=y[:], op=ALU.mult), r=[x, y], w=[z])
                DMA("sp", sd.zbf[rows, t0:t0 + TP], z[:], r=[z])
    else:
        hw = ph.sb([128, 4], F32)
        DMA("sp", hw[:], env.hyena_out_norm_w[l, :].rearrange("(c p) -> p c", p=128), w=[hw])
        ones = env.onesb
        ys = [ph.sb([128, TP], F32) for _ in range(4)]
        vs = [ph.sb([128, TP], BF16) for _ in range(4)]
        xs = [ph.sb([128, TP], BF16) for _ in range(4)]
        ghs = [ph.sb([128, TP], BF16) for _ in range(4)]
        sqb = [ph.sb([128, TP], BF16) for _ in range(4)]
        rsd = ph.sb([128, TP], F32)
        obf = [ph.sb([128, TP], BF16) for _ in range(4)]
        pss = [ph.ps([128, 512], F32) for _ in range(2)]
        for pc in range(NP):
            t0 = pc * TP
            for c in range(4):
                y, v, x, gh = ys[c], vs[c], xs[c], ghs[c]
                rows = slice(c * 128, (c + 1) * 128)
                eng = "dve"
                DMA("sp", y[:], sd.ybuf[rows, t0:t0 + TP], w=[y])
                DMA("sp", v[:], vsrc[rows, t0:t0 + TP], w=[v])
                DMA("sp", x[:], sd.ucT[512 + c * 128:512 + (c + 1) * 128, t0:t0 + TP], w=[x])
                DMA("pool", gh[:], sd.ghT[rows, t0:t0 + TP], w=[gh])
                OP(eng, lambda e, y=y, c=c: e.tensor_scalar(out=y[:], in0=y[:], scalar1=rn[:, 4 + c:5 + c], scalar2=None, op0=ALU.mult), r=[y, rn], w=[y])
                OP(eng, lambda e, y=y, v=v, c=c: e.scalar_tensor_tensor(out=y[:], in0=v[:], scalar=hbz[:, 1, c:c + 1], in1=y[:], op0=ALU.mult, op1=ALU.add),
                   r=[y, v, hbz], w=[y])
                OP(eng, lambda e, y=y, x=x: e.tensor_tensor(out=y[:], in0=x[:], in1=y[:], op=ALU.mult), r=[x, y], w=[y])
                OP(eng, lambda e, y=y, c=c: e.tensor_tensor(out=sqb[c][:], in0=y[:], in1=y[:], op=ALU.mult), r=[y], w=[sqb[c]])
            for sblk in range(TP // 512):
                sl = slice(sblk * 512, (sblk + 1) * 512)
                p = pss[sblk % 2]
                for c in range(4):
                    OP("pe", lambda e, c=c, p=p, sl=sl: e.matmul(p[:], lhsT=ones[:], rhs=sqb[c][:, sl], start=(c == 0), stop=(c == 3)),
                       r=[ones, sqb[c]], w=[p])
                OP("dve", lambda e, p=p, sl=sl: e.tensor_scalar(out=rsd[:, sl], in0=p[:], scalar1=1.0 / 512, scalar2=EPS, op0=ALU.mult, op1=ALU.add),
                   r=[p], w=[rsd])
            OP("act", lambda e: e.activation(out=rsd[:], in_=rsd[:], func=AF.Sqrt), r=[rsd], w=[rsd])
            OP("dve", lambda e: e.reciprocal(out=rsd[:], in_=rsd[:]), r=[rsd], w=[rsd])
            for c in range(4):
                y, gh, ob = ys[c], ghs[c], obf[c]
                eng = "dve"
                OP(eng, lambda e, y=y, c=c: e.scalar_tensor_tensor(out=y[:], in0=y[:], scalar=hw[:, c:c + 1], in1=rsd[:], op0=ALU.mult, op1=ALU.mult),
                   r=[y, hw, rsd], w=[y])
                OP(eng, lambda e, y=y, gh=gh, ob=ob: e.tensor_tensor(out=ob[:], in0=y[:], in1=gh[:], op=ALU.mult), r=[y, gh], w=[ob])
                DMA("sp", sd.oT[4 + c, :, t0:t0 + TP], ob[:], r=[ob])
    ph.end()


def phase_outproj(env, sd, l, last):
    OP, DMA = mk_ops(env)
    L, NT = sd.L, sd.NT
    x_src = sd.x_in if l == 0 else sd.xres
    ph = Phase(env, "p4")
    fg = None if last else filters_gen(env, sd, l + 1, ph)
    TPf = min(L, 2048)
    n_fy = (L // TPf) * ((TPf // 512) * 17) + 1
    f_per_tile = -(-n_fy // (L // 128)) if fg is not None else 0
    wob = ph.sb([128, 8, D_MODEL], BF16)
    wbb = [ph.alias(wob) for _ in range(8)]
    for dc in range(8):
        DMA("sp" if dc % 2 == 0 else "act", wob[:, dc, :], env.woutb[l, :, dc, :], w=[wbb[dc]])
    fw = None
    if last:
        fw = ph.sb([128, D_MODEL], F32)
        DMA("sp", fw[:], env.final_norm_w.rearrange("(o d) -> o d", o=1).partition_broadcast(128), w=[fw])
    oTs = [ph.sb([128, 8, 512], BF16) for _ in range(2)]
    xts = [ph.sb([128, D_MODEL], F32) for _ in range(3)]
    xns = [ph.sb([128, D_MODEL], F32) for _ in range(3)]
    sqscr = ph.sb([128, D_MODEL], F32)
    ss = [ph.sb([128, 1], F32) for _ in range(2)]
    pss = [ph.ps([128, 512], F32) for _ in range(4)]
    NG = L // 512
    cnt = 0
    for g in range(NG):
        oTt = oTs[g % 2]
        DMA("sp", oTt[:], sd.oT[:, :, g * 512:(g + 1) * 512].rearrange("c p t -> p c t"), w=[oTt])
        for i in range(4):
            r0 = (g * 4 + i) * 128
            xt = xts[cnt % 3]
            xn = xns[cnt % 3]
            s1 = ss[cnt % 2]
            DMA("pool", xt[:], x_src[r0:r0 + 128, :], w=[xt])
            for hf in range(2):
                p = pss[(cnt * 2 + hf) % 4]
                for c in range(8):
                    OP("pe", lambda e, c=c, i=i, hf=hf, p=p, oTt=oTt: e.matmul(p[:], lhsT=oTt[:, c, i * 128:(i + 1) * 128],
                                                                              rhs=wob[:, c, hf * 512:(hf + 1) * 512], start=(c == 0), stop=(c == 7)),
                       r=[oTt, wbb[c]], w=[p])
                OP("dve", lambda e, hf=hf, p=p, xt=xt, xn=xn: e.tensor_tensor(out=xn[:, hf * 512:(hf + 1) * 512], in0=p[:],
                                                                             in1=xt[:, hf * 512:(hf + 1) * 512], op=ALU.add), r=[p, xt], w=[xn])
            if not last:
                DMA("sp", sd.xres[r0:r0 + 128, :], xn[:], r=[xn])
            else:
                OP("pool", lambda e, s1=s1: e.memset(s1[:], 0.0), w=[s1])
                OP("act", lambda e, xn=xn, s1=s1: e.activation(out=sqscr[:], in_=xn[:], func=AF.Square, accum_out=s1[:, 0:1]), r=[xn, s1], w=[sqscr, s1])
                OP("dve", lambda e, s1=s1: e.tensor_scalar(out=s1[:], in0=s1[:], scalar1=1.0 / D_MODEL, scalar2=EPS, op0=ALU.mult, op1=ALU.add), r=[s1], w=[s1])
                OP("act", lambda e, s1=s1: e.activation(out=s1[:], in_=s1[:], func=AF.Sqrt), r=[s1], w=[s1])
                OP("dve", lambda e, s1=s1: e.reciprocal(out=s1[:], in_=s1[:]), r=[s1], w=[s1])
                OP("dve", lambda e, xn=xn, s1=s1: e.scalar_tensor_tensor(out=xn[:], in0=xn[:], scalar=s1[:, 0:1], in1=fw[:], op0=ALU.mult, op1=ALU.mult),
                   r=[xn, s1, fw], w=[xn])
                DMA("sp", sd.y_out[r0:r0 + 128, :], xn[:], r=[xn])
            cnt += 1
            for _ in range(f_per_tile):
                next(fg, None)
    if fg is not None:
        for _ in fg:
            pass
    ph.end()


WEIGHT_SPECS = [
    ("norm_w", (DEPTH, 1024)), ("w_in", (DEPTH, 1024, DIN)), ("q_norm_w", (DEPTH, 64)), ("k_norm_w", (DEPTH, 64)),
    ("conv_w", (DEPTH, 3, 1536)), ("conv_b", (DEPTH, 1536)), ("filt_w1", (DEPTH, 33, 64)), ("filt_b1", (DEPTH, 64)),
    ("filt_w2", (DEPTH, 64, 64)), ("filt_b2", (DEPTH, 64)), ("filt_w3", (DEPTH, 64, 2048)), ("filt_freq", (DEPTH, 64)),
    ("filt_decay", (DEPTH, 2048)), ("hyena_bias", (DEPTH, 2, 512)), ("attn_out_norm_w", (DEPTH, 512)),
    ("hyena_out_norm_w", (DEPTH, 512)), ("w_out", (DEPTH, 1024, 1024)), ("final_norm_w", (1024,)),
]


def build_program(seq_lens, depth=DEPTH, stop_after=None, debug=False):
    nc = bass.Bass("TRN2", target_bir_lowering=False)
    env = Env()
    env.nc = nc
    env.S = Sched(nc)
    OP, DMA = mk_ops(env)
    for name, shp in WEIGHT_SPECS:
        setattr(env, name, nc.dram_tensor(name, list(shp), F32, kind="ExternalInput").ap())
    env.winb = nc.dram_tensor("winb_scr", [DEPTH, 128, 8, DIN], BF16, kind="Internal").ap()
    env.woutb = nc.dram_tensor("woutb_scr", [DEPTH, 128, 8, D_MODEL], BF16, kind="Internal").ap()
    identb_d = nc.dram_tensor("identb", [128, 128], BF16, kind="ExternalInput").ap()
    identf_d = nc.dram_tensor("identf", [128, 128], F32, kind="ExternalInput").ap()
    consts = {}
    seqs = []
    for si, L in enumerate(seq_lens):
        sd = Env()
        sd.L = L
        sd.NT = L // 128
        sd.cfg = fft_cfg(L)
        NT, K1r, K1, CPB, CB, batches = sd.cfg
        if L not in consts:
            cs = make_consts(L)
            consts[L] = {}
            for k, v in cs.items():
                dt = BF16 if v.dtype == ml_dtypes.bfloat16 else F32
                consts[L][k] = nc.dram_tensor(f"c{L}_{k}", list(v.shape), dt, kind="ExternalInput").ap()
        sd.c = consts[L]
        sd.x_in = nc.dram_tensor(f"x{si}", [L, D_MODEL], F32, kind="ExternalInput").ap()
        sd.y_out = nc.dram_tensor(f"y{si}", [L, D_MODEL], F32, kind="ExternalOutput").ap()

        def scr(nm, shp, dt):
            return nc.dram_tensor(f"s{si}_{nm}", list(shp), dt, kind=("ExternalOutput" if debug else "Internal")).ap()
        sd.xres = scr("xres", [L, D_MODEL], F32)
        sd.qT = scr("qT", [5, 128, L], BF16)
        sd.vtok = scr("vtok", [L, 128], BF16)
        sd.ga = scr("ga", [L, 512], BF16)
        sd.uT = scr("uT", [1536, L], BF16)
        sd.ghT = scr("ghT", [512, L], BF16)
        sd.ucT = scr("ucT", [1024, L], BF16)
        sd.vbf = scr("vbf", [512, L], BF16)
        sd.zbf = scr("zbf", [512, L], BF16)
        sd.ybuf = scr("ybuf", [512, L], F32)
        sd.hbf = scr("hbf", [2, 2, 512, L], BF16)
        sd.Hd = scr("Hd", [2, len(batches), 2, 128, CB * K1], BF16)
        sd.oT = scr("oT", [8, 128, L], BF16)
        seqs.append(sd)
    env.stopped = False
    with ExitStack() as es:
        es.enter_context(nc.allow_non_contiguous_dma("param / layout loads"))

        def psb(name, shape, dt):
            return T(es.enter_context(nc.sbuf_tensor(name, shape, dt)), env.S.buf())
        env.identb = psb("identb_s", [128, 128], BF16)
        env.identf = psb("identf_s", [128, 128], F32)
        env.onesb = psb("onesb_s", [128, 128], BF16)
        env.rnorm = [psb("rnorm_s0", [128, 8], F32), psb("rnorm_s1", [128, 8], F32)]
        DMA("sp", env.identb[:], identb_d, w=[env.identb])
        DMA("sp", env.identf[:], identf_d, w=[env.identf])
        OP("pool", lambda e: e.memset(env.onesb[:], 1.0), w=[env.onesb])
        env.S.flush()
        phase_prep(env, depth)
        for sd in seqs:
            for l in range(depth):
                last = (l == depth - 1)
                steps = [
                    ("inproj", lambda: phase_inproj(env, sd, l)),
                    ("attn", lambda: phase_attn(env, sd, l)),
                    ("filters", (lambda: phase_filters(env, sd, l)) if l == 0 else (lambda: None)),
                    ("filter_fft", lambda: phase_filter_fft(env, sd, l)),
                    ("fftconv0", lambda: phase_fftconv(env, sd, l, 0)),
                    ("gate0", lambda: phase_gate(env, sd, l, 0)),
                    ("fftconv1", lambda: phase_fftconv(env, sd, l, 1)),
                    ("gate1", lambda: phase_gate(env, sd, l, 1)),
                    ("outproj", lambda: phase_outproj(env, sd, l, last)),
                ]
                for nm, fn in steps:
                    if env.stopped:
                        break
                    fn()
                    if stop_after == nm:
                        env.stopped = True
    env.consts_np = {L: make_consts(L) for L in consts}
    return nc, env


def make_in_maps(inputs, seq_lens, xs_per_core):
    base = {name: np.ascontiguousarray(np.asarray(inputs[name], dtype=np.float32)) for name, _ in WEIGHT_SPECS}
    base["identb"] = np.eye(128, dtype=np.float32).astype(ml_dtypes.bfloat16)
    base["identf"] = np.eye(128, dtype=np.float32)
    for L in set(seq_lens):
        for k, v in make_consts(L).items():
            base[f"c{L}_{k}"] = np.ascontiguousarray(v)
    maps = []
    for xs in xs_per_core:
        m = dict(base)
        for si, x in enumerate(xs):
            m[f"x{si}"] = np.ascontiguousarray(x, dtype=np.float32)
        maps.append(m)
    return maps


_PROG_CACHE = {}


def kernel(**inputs):
    xp = np.asarray(inputs["x_prompt"], dtype=np.float32)
    xs = np.asarray(inputs["x_sample"], dtype=np.float32)
    B = xp.shape[0]
    seq_lens = (xp.shape[1], xs.shape[1])
    if seq_lens not in _PROG_CACHE:
        _PROG_CACHE[seq_lens] = build_program(list(seq_lens))[0]
    nc = _PROG_CACHE[seq_lens]
    maps = make_in_maps(inputs, seq_lens, [[xp[b], xs[b]] for b in range(B)])
    res = run_bass_kernel_spmd(nc, maps, core_ids=list(range(B)))
    yp = np.stack([np.asarray(r["y0"], dtype=np.float32) for r in res.results], 0)
    ys = np.stack([np.asarray(r["y1"], dtype=np.float32) for r in res.results], 0)
    return (yp, ys)
```
